# Optimizing a Trainium2 kernel written in Bass

```python
import math
import jax
import jax.numpy as jnp
from jax import lax
import numpy as np

D_MODEL = 1024
BATCH = 8
SEQ = 8192
DEPTH = 2
DEC_BATCH = 16
DEC_SEQ = 2048
PAST_LEN = 128

HEAD_DIM = 64
GRID_W = 64
NA_HEADS = 8
NA_WIN_ROWS = 8
NA_WIN_COLS = 16
NA_WIDTH = NA_HEADS * HEAD_DIM
HY_WIDTH = D_MODEL // 2
HY_ORDER = 2
HY_DIRS = 2
HY_SHORT_CONV = 3
HY_POS_BANDS = 16
HY_EMB_DIM = 1 + 2 * HY_POS_BANDS
HY_FILTER_HIDDEN = 64
HY_FAST_DECAY_PCT = 0.3
HY_SLOW_DECAY_PCT = 1.5
HY_DECAY_TARGET = 1e-2
SWA_HEADS = 8
SWA_KV_HEADS = 2
SWA_WIDTH = SWA_HEADS * HEAD_DIM
SWA_KV_WIDTH = SWA_KV_HEADS * HEAD_DIM
SWA_WINDOW = 128
SWA_BLOCK = 128
N_BRANCH = 3
D_FF = 4 * D_MODEL
OFF_HY = 3 * NA_WIDTH
OFF_SWA = OFF_HY + 3 * HY_WIDTH
OFF_GATE = OFF_SWA + SWA_WIDTH + 2 * SWA_KV_WIDTH
D_IN = OFF_GATE + N_BRANCH * D_MODEL
DEEPNORM_ALPHA = (2 * DEPTH) ** 0.25
DEEPNORM_BETA = (8 * DEPTH) ** -0.25
LN_EPS = 1e-5
NEG_BIG = -1e30

kernel_name = "hybrid_bidir_encoder_two_groups"


def _layernorm(x, g, b):
    xf = x.astype(jnp.float32)
    mu = jnp.mean(xf, axis=-1, keepdims=True)
    var = jnp.mean(jnp.square(xf - mu), axis=-1, keepdims=True)
    return ((xf - mu) * lax.rsqrt(var + LN_EPS) * g + b).astype(x.dtype)


def _neighbourhood_attention(q, k, v, rpb):
    b_sz, seq_len, heads, dh = q.shape
    rows = seq_len // GRID_W
    kr = min(NA_WIN_ROWS, rows)
    kc = NA_WIN_COLS
    qg = q.reshape(b_sz * rows, GRID_W, heads, dh)
    kg = k.reshape(b_sz, rows, GRID_W, heads, dh)
    vg = v.reshape(b_sz, rows, GRID_W, heads, dh)
    cols = jnp.arange(GRID_W)
    col_start = jnp.clip(cols - kc // 2, 0, GRID_W - kc)
    col_idx = col_start[:, None] + jnp.arange(kc)[None, :]
    rpb_cols = rpb[:, :, col_idx - cols[:, None] + (NA_WIN_COLS - 1)]
    scale = dh ** -0.5

    def row_block(idx):
        bi = idx // rows
        r = idx % rows
        rs = jnp.clip(r - kr // 2, 0, rows - kr)
        q_r = qg[idx]
        k_r = lax.dynamic_slice(kg, (bi, rs, 0, 0, 0), (1, kr, GRID_W, heads, dh))[0]
        v_r = lax.dynamic_slice(vg, (bi, rs, 0, 0, 0), (1, kr, GRID_W, heads, dh))[0]
        k_n = k_r[:, col_idx]
        v_n = v_r[:, col_idx]
        bias = jnp.transpose(rpb_cols[:, rs + jnp.arange(kr) - r + (NA_WIN_ROWS - 1)], (0, 2, 1, 3))
        s = jnp.einsum('whd,rwkhd->hwrk', q_r, k_n, preferred_element_type=jnp.float32) * scale + bias
        p = jax.nn.softmax(s, axis=(-2, -1))
        return jnp.einsum('hwrk,rwkhd->whd', p.astype(v.dtype), v_n)

    out = lax.map(row_block, jnp.arange(b_sz * rows))
    return out.reshape(b_sz, seq_len, heads * dh)


def _hyena_filters(seq_len, w1, b1, w2, b2, freq, w3):
    f32 = jnp.float32
    t = jnp.linspace(0.0, 1.0, seq_len, dtype=f32)[:, None]
    w = 2.0 * math.pi * jnp.arange(seq_len, dtype=f32) / seq_len
    bands = jnp.linspace(1e-4, HY_POS_BANDS - 1, HY_POS_BANDS, dtype=f32)
    ang = w[:, None] * bands[None, :]
    z = jnp.concatenate([t, jnp.cos(ang), -jnp.sin(ang)], axis=-1)
    fr = freq.astype(f32)
    h = jnp.sin(fr * (z @ w1.astype(f32) + b1.astype(f32)))
    h = jnp.sin(fr * (h @ w2.astype(f32) + b2.astype(f32)))
    h = h @ w3.astype(f32)
    min_decay = math.log(HY_DECAY_TARGET) / HY_SLOW_DECAY_PCT
    max_decay = math.log(HY_DECAY_TARGET) / HY_FAST_DECAY_PCT
    deltas = jnp.linspace(min_decay, max_decay, HY_WIDTH, dtype=f32)
    window = jnp.exp(-t * jnp.abs(deltas)[None, :])
    return h.reshape(seq_len, HY_ORDER, HY_DIRS, HY_WIDTH) * window[:, None, None, :]


def _bidir_fftconv(u, h_fwd, h_bwd):
    seq_len, ch = h_fwd.shape
    n = 2 * seq_len
    kern = jnp.concatenate([h_fwd, jnp.zeros((1, ch), jnp.float32), h_bwd[:0:-1]], axis=0)
    kf = jnp.fft.rfft(kern, n=n, axis=0)
    uf = jnp.fft.rfft(u.astype(jnp.float32), n=n, axis=1)
    y = jnp.fft.irfft(uf * kf[None], n=n, axis=1)[:, :seq_len]
    return y.astype(u.dtype)


def _hyena(proj, conv_w, conv_b, filters, bias):
    seq_len = proj.shape[1]
    half = HY_SHORT_CONV // 2
    xp = jnp.pad(proj, ((0, 0), (half, half), (0, 0)))
    u = sum(xp[:, j:j + seq_len] * conv_w[j] for j in range(HY_SHORT_CONV)) + conv_b
    v, x1, x2 = jnp.split(u, 3, axis=-1)
    z = v
    for n, gate in enumerate((x1, x2)):
        z = gate * (_bidir_fftconv(z, filters[:, n, 0], filters[:, n, 1]) + bias[n] * z)
    return z


def _window_gqa(q, k, v, sink):
    b_sz, seq_len, hq, dh = q.shape
    hkv = k.shape[2]
    grp = hq // hkv
    nb = seq_len // SWA_BLOCK
    span = SWA_BLOCK + 2 * SWA_WINDOW
    pad = ((0, 0), (SWA_WINDOW, SWA_WINDOW), (0, 0), (0, 0))
    kp = jnp.pad(k, pad)
    vp = jnp.pad(v, pad)
    slopes = 2.0 ** (-8.0 * (jnp.arange(hq, dtype=jnp.float32) + 1.0) / hq)
    q_off = jnp.arange(SWA_BLOCK)[:, None]
    k_off = jnp.arange(span) - SWA_WINDOW
    rel = jnp.abs(k_off[None, :] - q_off)
    band = rel <= SWA_WINDOW
    alibi = (-slopes[:, None, None] * rel.astype(jnp.float32)).reshape(hkv, grp, SWA_BLOCK, span)
    sink_l = sink.astype(jnp.float32).reshape(hkv, grp)[None, :, :, None, None]
    scale = dh ** -0.5

    def block(i):
        start = i * SWA_BLOCK
        q_b = lax.dynamic_slice_in_dim(q, start, SWA_BLOCK, axis=1).reshape(b_sz, SWA_BLOCK, hkv, grp, dh)
        k_b = lax.dynamic_slice_in_dim(kp, start, span, axis=1)
        v_b = lax.dynamic_slice_in_dim(vp, start, span, axis=1)
        key_pos = start + k_off
        valid = band & ((key_pos >= 0) & (key_pos < seq_len))[None, :]
        s = jnp.einsum('bqgjd,bkgd->bgjqk', q_b, k_b, preferred_element_type=jnp.float32) * scale + alibi
        s = jnp.where(valid, s, NEG_BIG)
        m = jnp.maximum(jnp.max(s, axis=-1, keepdims=True), sink_l)
        p = jnp.exp(s - m)
        denom = jnp.sum(p, axis=-1, keepdims=True) + jnp.exp(sink_l - m)
        o = jnp.einsum('bgjqk,bkgd->bqgjd', (p / denom).astype(v.dtype), v_b)
        return o.reshape(b_sz, SWA_BLOCK, hq * dh)

    out = lax.map(block, jnp.arange(nb))
    return jnp.transpose(out, (1, 0, 2, 3)).reshape(b_sz, seq_len, hq * dh)


def _encoder_block(x, w_in, b_in, hy_conv_w, hy_conv_b, hy_filt_w1, hy_filt_b1, hy_filt_w2, hy_filt_b2,
                   hy_filt_freq, hy_filt_w3, hy_bias, na_rpb, swa_sink, w_branch_a, w_branch_b, w_branch_c,
                   w_out, ln1_g, ln1_b, w_up, b_up, w_down, b_down, ln2_g, ln2_b):
    b_sz, seq_len, _ = x.shape
    h = jnp.einsum('bld,de->ble', x, w_in) + b_in
    splits = [NA_WIDTH, 2 * NA_WIDTH, OFF_HY, OFF_SWA, OFF_SWA + SWA_WIDTH,
              OFF_SWA + SWA_WIDTH + SWA_KV_WIDTH, OFF_GATE]
    na_q, na_k, na_v, hy_in, sw_q, sw_k, sw_v, gates = jnp.split(h, splits, axis=-1)
    a = _neighbourhood_attention(na_q.reshape(b_sz, seq_len, NA_HEADS, HEAD_DIM),
                                 na_k.reshape(b_sz, seq_len, NA_HEADS, HEAD_DIM),
                                 na_v.reshape(b_sz, seq_len, NA_HEADS, HEAD_DIM), na_rpb)
    filters = _hyena_filters(seq_len, hy_filt_w1, hy_filt_b1, hy_filt_w2, hy_filt_b2, hy_filt_freq, hy_filt_w3)
    hb = _hyena(hy_in, hy_conv_w, hy_conv_b, filters, hy_bias)
    c = _window_gqa(sw_q.reshape(b_sz, seq_len, SWA_HEADS, HEAD_DIM),
                    sw_k.reshape(b_sz, seq_len, SWA_KV_HEADS, HEAD_DIM),
                    sw_v.reshape(b_sz, seq_len, SWA_KV_HEADS, HEAD_DIM), swa_sink)
    g = jax.nn.sigmoid(gates.astype(jnp.float32)).astype(x.dtype).reshape(b_sz, seq_len, N_BRANCH, D_MODEL)
    merged = (g[:, :, 0] * (a @ w_branch_a) + g[:, :, 1] * (hb @ w_branch_b) + g[:, :, 2] * (c @ w_branch_c))
    mix = merged @ w_out
    x = _layernorm(DEEPNORM_ALPHA * x + mix, ln1_g, ln1_b)
    ff = jnp.square(jax.nn.relu(x @ w_up + b_up)) @ w_down + b_down
    return _layernorm(DEEPNORM_ALPHA * x + ff, ln2_g, ln2_b)


def setup_inputs(seed: int = 0) -> dict:
    key = jax.random.key(seed)
    ks = jax.random.split(key, 32)
    f32 = jnp.float32

    def nrm(k, shape, scale):
        return jax.random.normal(k, shape, f32) * scale

    return {
        'x_prompt': nrm(ks[0], (BATCH, SEQ, D_MODEL), 1.0),
        'x_sample': nrm(ks[1], (DEC_BATCH, DEC_SEQ, D_MODEL), 1.0),
        'w_in': nrm(ks[2], (DEPTH, D_MODEL, D_IN), D_MODEL ** -0.5),
        'b_in': nrm(ks[3], (DEPTH, D_IN), 0.02),
        'hy_conv_w': nrm(ks[4], (DEPTH, HY_SHORT_CONV, 3 * HY_WIDTH), HY_SHORT_CONV ** -0.5),
        'hy_conv_b': nrm(ks[5], (DEPTH, 3 * HY_WIDTH), 0.02),
        'hy_filt_w1': nrm(ks[6], (DEPTH, HY_EMB_DIM, HY_FILTER_HIDDEN), HY_EMB_DIM ** -0.5),
        'hy_filt_b1': nrm(ks[7], (DEPTH, HY_FILTER_HIDDEN), 0.02),
        'hy_filt_w2': nrm(ks[8], (DEPTH, HY_FILTER_HIDDEN, HY_FILTER_HIDDEN), HY_FILTER_HIDDEN ** -0.5),
        'hy_filt_b2': nrm(ks[9], (DEPTH, HY_FILTER_HIDDEN), 0.02),
        'hy_filt_freq': 1.0 + nrm(ks[10], (DEPTH, HY_FILTER_HIDDEN), 0.01),
        'hy_filt_w3': nrm(ks[11], (DEPTH, HY_FILTER_HIDDEN, HY_ORDER * HY_DIRS * HY_WIDTH), 0.08 * HY_FILTER_HIDDEN ** -0.5),
        'hy_bias': nrm(ks[12], (DEPTH, HY_ORDER, HY_WIDTH), 0.5),
        'na_rpb': nrm(ks[13], (DEPTH, NA_HEADS, 2 * NA_WIN_ROWS - 1, 2 * NA_WIN_COLS - 1), 0.02),
        'swa_sink': nrm(ks[14], (DEPTH, SWA_HEADS), 0.5),
        'w_branch_a': nrm(ks[15], (DEPTH, NA_WIDTH, D_MODEL), DEEPNORM_BETA * NA_WIDTH ** -0.5),
        'w_branch_b': nrm(ks[16], (DEPTH, HY_WIDTH, D_MODEL), DEEPNORM_BETA * HY_WIDTH ** -0.5),
        'w_branch_c': nrm(ks[17], (DEPTH, SWA_WIDTH, D_MODEL), DEEPNORM_BETA * SWA_WIDTH ** -0.5),
        'w_out': nrm(ks[18], (DEPTH, D_MODEL, D_MODEL), DEEPNORM_BETA * D_MODEL ** -0.5),
        'ln1_g': 1.0 + nrm(ks[19], (DEPTH, D_MODEL), 0.02),
        'ln1_b': nrm(ks[20], (DEPTH, D_MODEL), 0.02),
        'w_up': nrm(ks[21], (DEPTH, D_MODEL, D_FF), D_MODEL ** -0.5),
        'b_up': nrm(ks[22], (DEPTH, D_FF), 0.02),
        'w_down': nrm(ks[23], (DEPTH, D_FF, D_MODEL), DEEPNORM_BETA * D_FF ** -0.5),
        'b_down': nrm(ks[24], (DEPTH, D_MODEL), 0.02),
        'ln2_g': 1.0 + nrm(ks[25], (DEPTH, D_MODEL), 0.02),
        'ln2_b': nrm(ks[26], (DEPTH, D_MODEL), 0.02),
    }


def reference(x_prompt, x_sample, w_in, b_in, hy_conv_w, hy_conv_b, hy_filt_w1, hy_filt_b1, hy_filt_w2,
              hy_filt_b2, hy_filt_freq, hy_filt_w3, hy_bias, na_rpb, swa_sink, w_branch_a, w_branch_b,
              w_branch_c, w_out, ln1_g, ln1_b, w_up, b_up, w_down, b_down, ln2_g, ln2_b):
    y_prompt = x_prompt
    y_sample = x_sample
    for l in range(DEPTH):
        layer = (w_in[l], b_in[l], hy_conv_w[l], hy_conv_b[l], hy_filt_w1[l], hy_filt_b1[l], hy_filt_w2[l],
                 hy_filt_b2[l], hy_filt_freq[l], hy_filt_w3[l], hy_bias[l], na_rpb[l], swa_sink[l],
                 w_branch_a[l], w_branch_b[l], w_branch_c[l], w_out[l], ln1_g[l], ln1_b[l], w_up[l], b_up[l],
                 w_down[l], b_down[l], ln2_g[l], ln2_b[l])
        y_prompt = _encoder_block(y_prompt, *layer)
        y_sample = _encoder_block(y_sample, *layer)
    return (y_prompt, y_sample)
```

```python
import math
from contextlib import ExitStack

import numpy as np
import concourse.bass as bass
import concourse.mybir as mybir
from concourse.bass_utils import run_bass_kernel_spmd

F32 = mybir.dt.float32
BF16 = mybir.dt.bfloat16
AF = mybir.ActivationFunctionType
ALU = mybir.AluOpType

D_MODEL = 1024
DEPTH = 2
HEAD_DIM = 64
GRID_W = 64
NA_HEADS = 8
NA_WIN_ROWS = 8
NA_WIN_COLS = 16
NA_WIDTH = 512
HY_WIDTH = 512
HY_FILTER_HIDDEN = 64
HY_EMB_DIM = 33
HY_POS_BANDS = 16
SWA_HEADS = 8
SWA_KV_HEADS = 2
SWA_WIDTH = 512
SWA_KV_WIDTH = 128
SWA_WINDOW = 128
D_FF = 4096
OFF_HY = 3 * NA_WIDTH
OFF_SWA = OFF_HY + 3 * HY_WIDTH
OFF_GATE = OFF_SWA + SWA_WIDTH + 2 * SWA_KV_WIDTH
D_IN = OFF_GATE + 3 * D_MODEL
ALPHA = (2 * DEPTH) ** 0.25
LN_EPS = 1e-5
NEG = -30000.0

SAME_ENGINE_SYNC = True
GLOB = [None]


class Part:
    __slots__ = ("key", "sub")

    def __init__(self, key, sub):
        self.key = key
        self.sub = sub


class _Op:
    __slots__ = ("eng", "emit", "deps", "is_dma", "signal", "sem", "val", "idx", "prev_val", "raw")

    def __init__(self, eng, emit, is_dma, idx):
        self.eng = eng
        self.emit = emit
        self.deps = []
        self.is_dma = is_dma
        self.signal = is_dma
        self.sem = None
        self.val = 0
        self.prev_val = 0
        self.idx = idx
        self.raw = set()


class Sched:
    ENGS = ("pe", "act", "dve", "pool", "sp")
    NDMA = {"sp": 16, "act": 12, "pool": 8}

    def __init__(self, nc, name, glob=None):
        self.nc = nc
        self.name = name
        self.ops = []
        self.state = {}
        self.glob = glob if glob is not None else GLOB[0]

    @classmethod
    def make_glob(cls, nc, es):
        csem = {e: es.enter_context(nc.semaphore(f"c_{e}")) for e in ("pe", "act", "dve", "pool")}
        dsem = {q: [es.enter_context(nc.semaphore(f"d_{q}{i}")) for i in range(n)] for q, n in cls.NDMA.items()}
        return {"csem": csem, "dsem": dsem, "cnt": {e: 0 for e in csem},
                "duse": {q: [0] * n for q, n in cls.NDMA.items()}, "drr": {q: 0 for q in cls.NDMA}}

    @staticmethod
    def _conf(a, b):
        return a is None or b is None or a == b

    def _touch(self, op, item, write, rd_excl=False):
        if isinstance(item, Part):
            key, sub = item.key, item.sub
        else:
            key, sub = item, None
        st = self.state.get(key)
        if st is None:
            st = self.state[key] = {"w": [], "r": []}
        for (s, o) in st["w"]:
            if self._conf(s, sub):
                op.deps.append(o)
                if not (write and not rd_excl):
                    op.raw.add(id(o))
        if write:
            for (s, o) in st["r"]:
                if self._conf(s, sub):
                    op.deps.append(o)
            if sub is None:
                st["w"] = [(None, op)]
                st["r"] = []
            else:
                st["w"] = [(s, o) for (s, o) in st["w"] if s != sub] + [(sub, op)]
                st["r"] = [(s, o) for (s, o) in st["r"] if s != sub]
        else:
            if not op.is_dma:
                st["r"] = [(s, o) for (s, o) in st["r"]
                           if not (s == sub and o.eng == op.eng and not o.is_dma)]
            st["r"].append((sub, op))

    PSUM_KEYS = ("acc", "pT", "S", "O", "pm", "p3", "psA", "psX", "psB", "psY")

    def _is_psum(self, item):
        key = item.key if isinstance(item, Part) else item
        name = key[0] if isinstance(key, tuple) else key
        return name in self.PSUM_KEYS

    def add(self, eng, emit, reads=(), writes=(), dma=False):
        op = _Op(eng, emit, dma, len(self.ops))
        for r in reads:
            if self._is_psum(r):
                k = r.key if isinstance(r, Part) else r
                self._touch(op, k, True, rd_excl=True)
            else:
                self._touch(op, r, False)
        for w in writes:
            self._touch(op, w, True)
        self.ops.append(op)
        return op

    def dma(self, queue, out, in_, reads=(), writes=(), **kw):
        return self.add(queue, lambda e, o=out, i=in_: e.dma_start(out=o, in_=i, **kw), reads, writes, dma=True)

    def run(self):
        nc = self.nc
        per = {e: [] for e in self.ENGS}
        for op in self.ops:
            seen = set()
            deps = []
            for d in op.deps:
                if d is op or id(d) in seen:
                    continue
                seen.add(id(d))
                deps.append(d)
            op.deps = deps
            per[op.eng].append(op)
            for d in deps:
                if d.is_dma:
                    continue
                if d.eng == op.eng and not op.is_dma and (op.eng == "pe" or not SAME_ENGINE_SYNC or id(d) not in op.raw):
                    continue
                d.signal = True
        for e in self.ENGS:
            for op in reversed(per[e]):
                if not op.is_dma:
                    op.signal = True
                    break
        with ExitStack() as es:
            G = self.glob
            csem, dsem, cnt, duse, drr = G["csem"], G["dsem"], G["cnt"], G["duse"], G["drr"]
            for e in self.ENGS:
                for op in per[e]:
                    if op.is_dma:
                        k = drr[e]
                        drr[e] = (k + 1) % self.NDMA[e]
                        op.prev_val = 16 * duse[e][k]
                        duse[e][k] += 1
                        op.sem = dsem[e][k]
                        op.val = 16 * duse[e][k]
                    elif op.signal:
                        cnt[e] += 1
                        op.sem = csem[e]
                        op.val = cnt[e]
            finals = []
            for e in csem:
                if cnt[e]:
                    finals.append((csem[e], cnt[e]))
            for q in self.NDMA:
                for k, n in enumerate(duse[q]):
                    if n:
                        finals.append((dsem[q][k], 16 * n))
            block = es.enter_context(nc.Block())

            def body(ename):
                def f(eh):
                    waited = {}

                    def wait(sem, val):
                        if val <= 0:
                            return
                        if waited.get(id(sem), 0) >= val:
                            return
                        waited[id(sem)] = val
                        eh.wait_ge(sem, val)

                    for op in per[ename]:
                        for d in op.deps:
                            if (not d.is_dma) and d.eng == ename and not op.is_dma and \
                                    (ename == "pe" or not SAME_ENGINE_SYNC or id(d) not in op.raw):
                                continue
                            wait(d.sem, d.val)
                        if op.is_dma:
                            wait(op.sem, op.prev_val)
                        ins = op.emit(eh)
                        if op.signal:
                            ins.then_inc(op.sem, 16 if op.is_dma else 1)
                    for (sem, val) in finals:
                        wait(sem, val)
                return f

            block.tensor(body("pe"))
            block.scalar(body("act"))
            block.vector(body("dve"))
            block.gpsimd(body("pool"))
            block.sync(body("sp"))


def sbap(t, part0, nparts, off, dims):
    row = 1
    for s in list(t.shape)[1:]:
        row *= int(s)
    return bass.AP(t, part0 * row + off, [[row, nparts]] + [list(d) for d in dims])


class Cfg:
    def __init__(self, seq_lens=(8192, 2048, 2048), depth=DEPTH, mixers=("na", "hy", "swa")):
        self.seq_lens = tuple(seq_lens)
        self.depth = depth
        self.mixers = tuple(mixers)
        self.ulens = sorted(set(self.seq_lens), reverse=True)


FM_CHUNKS = []
for j in range(4):
    FM_CHUNKS.append(("qn", j, [(j * 128, 128)], "id"))
for j in range(4):
    FM_CHUNKS.append(("kn", j, [(NA_WIDTH + j * 128, 128)], "id"))
for j in range(12):
    FM_CHUNKS.append(("hy", j, [(OFF_HY + j * 128, 128)], "id"))
for j in range(4):
    FM_CHUNKS.append(("qs", j, [(OFF_SWA + j * 128, 128)], "id"))
for g in range(2):
    c0 = OFF_SWA + SWA_WIDTH + g * 64
    FM_CHUNKS.append(("ks", g, [(c0, 64), (c0, 64)], "id"))
for j in range(24):
    FM_CHUNKS.append(("g", j, [(OFF_GATE + j * 128, 128)], "sig"))
NFM = len(FM_CHUNKS)
TM_GROUPS = [
    ("vn", [(2 * NA_WIDTH, 512)]),
    ("vs", [(OFF_SWA + SWA_WIDTH + SWA_KV_WIDTH, 64), (OFF_SWA + SWA_WIDTH + SWA_KV_WIDTH, 64),
            (OFF_SWA + SWA_WIDTH + SWA_KV_WIDTH + 64, 64), (OFF_SWA + SWA_WIDTH + SWA_KV_WIDTH + 64, 64)]),
]
TM_OFF = [NFM * 128, NFM * 128 + 512]
TM_N = [512, 256]
WA_COLS = NFM * 128 + 512 + 256


def _col_index():
    idx = []
    for (_, _, rngs, _) in FM_CHUNKS:
        for (c0, n) in rngs:
            idx.extend(range(c0, c0 + n))
    for (_, rngs) in TM_GROUPS:
        for (c0, n) in rngs:
            idx.extend(range(c0, c0 + n))
    return np.asarray(idx, dtype=np.int64)


COL_INDEX = _col_index()


def kmajor(w):
    K, N = w.shape
    return np.ascontiguousarray(w.reshape(K // 128, 128, N).transpose(1, 0, 2))


def host_layout_weights(inp, depth):
    out = {}
    w_in = np.asarray(inp["w_in"], np.float32)
    b_in = np.asarray(inp["b_in"], np.float32)
    out["wA"] = np.stack([kmajor(w_in[l][:, COL_INDEX]) for l in range(depth)])
    bA = np.stack([b_in[l][COL_INDEX] for l in range(depth)])
    out["bA_fm"] = np.ascontiguousarray(bA[:, :NFM * 128].reshape(depth, NFM, 128).transpose(0, 2, 1))
    out["bA_tm"] = np.ascontiguousarray(bA[:, NFM * 128:].reshape(depth, 1, 768))
    for nm in ("w_branch_a", "w_branch_b", "w_branch_c", "w_out", "w_up", "w_down"):
        w = np.asarray(inp[nm], np.float32)
        out[nm] = np.stack([kmajor(w[l]) for l in range(depth)])
    out["b_up"] = np.ascontiguousarray(np.asarray(inp["b_up"], np.float32)[:depth].reshape(depth, 32, 128).transpose(0, 2, 1))
    f = lambda n: np.asarray(inp[n], np.float32)[:depth]
    out["hy_w1"] = np.ascontiguousarray(f("hy_filt_w1"))
    out["hy_w2"] = np.ascontiguousarray(f("hy_filt_w2"))
    out["hy_w3"] = np.ascontiguousarray(f("hy_filt_w3"))
    out["hy_b1"] = np.ascontiguousarray(f("hy_filt_b1").reshape(depth, 64, 1))
    out["hy_b2"] = np.ascontiguousarray(f("hy_filt_b2").reshape(depth, 64, 1))
    out["hy_fr"] = np.ascontiguousarray(f("hy_filt_freq").reshape(depth, 64, 1))
    out["hy_bias"] = np.ascontiguousarray(f("hy_bias").reshape(depth, 1, 1024))
    out["hy_cw"] = np.ascontiguousarray(f("hy_conv_w").reshape(depth, 3, 12, 128).transpose(0, 3, 2, 1))
    out["hy_cb"] = np.ascontiguousarray(f("hy_conv_b").reshape(depth, 12, 128).transpose(0, 2, 1))
    out["na_tab"] = host_na_tables(inp["na_rpb"], depth)
    out["swa_sink"] = np.ascontiguousarray(np.asarray(inp["swa_sink"], np.float32)[:depth].reshape(depth, 1, 8))
    for nm in ("b_down", "ln1_g", "ln1_b", "ln2_g", "ln2_b"):
        out[nm] = np.ascontiguousarray(np.asarray(inp[nm], np.float32)[:depth].reshape(depth, 1, 1024))
    return out


def _na_inwin():
    c = np.arange(64)[:, None]
    w = np.arange(64)[None, :]
    cs = np.clip(w - NA_WIN_COLS // 2, 0, GRID_W - NA_WIN_COLS)
    return (c >= cs) & (c < cs + NA_WIN_COLS)


def host_na_tables(rpb, depth):
    rpb = np.asarray(rpb, np.float32)
    inw = _na_inwin()
    c = np.arange(64)[:, None]
    w = np.arange(64)[None, :]
    cidx = np.clip(c - w + (NA_WIN_COLS - 1), 0, 2 * NA_WIN_COLS - 2)
    out = np.zeros((depth, 2, 128, NA_HEADS, 8, 64), np.float32)
    for l in range(depth):
        for t in range(2):
            for par in range(2):
                for m in range(8):
                    rho = 2 * m + par + t
                    if rho >= 2 * NA_WIN_ROWS - 1:
                        continue
                    g = rpb[l][:, rho, :][:, cidx]
                    g = np.where(inw[None], g, np.float32(0.0))
                    out[l, t, par * 64:(par + 1) * 64, :, m, :] = g.transpose(1, 0, 2)
    return out


def host_constants(cfg):
    c = {}
    import ml_dtypes
    bf = ml_dtypes.bfloat16
    c["ident"] = np.eye(128, dtype=np.float32).astype(bf)
    inw = _na_inwin()
    mk = np.where(inw, 0.0, NEG).astype(np.float32)
    c["na_mask"] = np.concatenate([mk, mk], axis=0)
    slopes = 2.0 ** (-8.0 * (np.arange(SWA_HEADS, dtype=np.float64) + 1.0) / SWA_HEADS)
    k = np.arange(128)[:, None, None, None]
    h = np.arange(8)[None, :, None, None]
    ch = np.arange(3)[None, None, :, None] - 1
    q = np.arange(128)[None, None, None, :]
    rel = np.abs(q - (k + 128 * ch))
    val = np.where(rel <= SWA_WINDOW, -slopes[h] * rel * (HEAD_DIM ** 0.5), NEG)
    hi = val.astype(np.float32).astype(bf)
    lo = (val - hi.astype(np.float64)).astype(np.float32).astype(bf)
    c["al_hi"] = hi
    c["al_lo"] = lo
    min_decay = math.log(1e-2) / 1.5
    max_decay = math.log(1e-2) / 0.3
    deltas = np.abs(np.linspace(min_decay, max_decay, HY_WIDTH, dtype=np.float32)).astype(np.float32)
    c["absdelta"] = np.ascontiguousarray(deltas.reshape(4, 128).T)
    for L in cfg.ulens:
        N1 = 2 * L // 128
        NK1 = N1 // 2 + 1
        Na = N1 // 2
        N = 2 * L
        pos = np.concatenate([np.arange(L), [0], np.arange(L - 1, 0, -1)]).astype(np.int64)
        t = np.linspace(0.0, 1.0, L, dtype=np.float32)
        wv = (2.0 * np.float32(math.pi) * np.arange(L, dtype=np.float32) / np.float32(L)).astype(np.float32)
        bands = np.linspace(1e-4, HY_POS_BANDS - 1, HY_POS_BANDS, dtype=np.float32)
        ang = (wv[:, None] * bands[None, :]).astype(np.float32)
        z = np.concatenate([t[:, None], np.cos(ang), -np.sin(ang)], axis=-1).astype(np.float32)
        c[f"zz{L}"] = np.ascontiguousarray(z[pos].T)
        tt = -t[pos]
        tt[L] = -1.0e4
        c[f"tneg{L}"] = np.ascontiguousarray(tt.reshape(1, 2 * L).astype(np.float32))
        a = np.arange(N1, dtype=np.float64)[:, None]
        k1 = np.arange(NK1, dtype=np.float64)[None, :]
        ang1 = 2 * np.pi * a * k1 / N1
        F1 = np.stack([np.cos(ang1), -np.sin(ang1)], axis=-1).reshape(N1, 2 * NK1)
        c[f"F1_{L}"] = F1.astype(np.float32).astype(bf)
        b = np.arange(128, dtype=np.float64)[:, None, None]
        k1 = np.arange(NK1, dtype=np.float64)[None, :, None]
        k2 = np.arange(128, dtype=np.float64)[None, None, :]
        angm = -2 * np.pi * b * (k1 + N1 * k2) / N
        c[f"Mr_{L}"] = np.cos(angm).astype(np.float32).astype(bf)
        c[f"Mi_{L}"] = np.sin(angm).astype(np.float32).astype(bf)
        k1 = np.arange(NK1, dtype=np.float64)[:, None, None]
        bb = np.arange(128, dtype=np.float64)[None, :, None]
        aa = np.arange(Na, dtype=np.float64)[None, None, :]
        ck = np.where((k1 == 0) | (k1 == N1 // 2), 1.0, 2.0) / N1
        angt = 2 * np.pi * (aa * k1 / N1 + bb * k1 / N)
        c[f"Tr_{L}"] = (ck * np.cos(angt)).astype(np.float32).astype(bf)
        c[f"Tin_{L}"] = (-ck * np.sin(angt)).astype(np.float32).astype(bf)
    k2 = np.arange(128, dtype=np.float64)[:, None]
    b = np.arange(128, dtype=np.float64)[None, :]
    angc = 2 * np.pi * b * k2 / 128
    Cr = np.cos(angc) / 128
    Ci = np.sin(angc) / 128
    c["C1"] = np.concatenate([Cr, Ci], axis=1).astype(np.float32).astype(bf)
    c["C2"] = np.concatenate([-Ci, Cr], axis=1).astype(np.float32).astype(bf)
    return c


class Builder:
    def __init__(self, cfg):
        self.cfg = cfg
        self.nc = bass.Bass("TRN2", target_bir_lowering=False)
        self.ext_in = {}
        self.dram = {}

    def inp(self, name, shape, dt):
        t = self.nc.dram_tensor(name, list(shape), dt, kind="ExternalInput").ap()
        self.ext_in[name] = t
        return t

    def outp(self, name, shape, dt):
        return self.nc.dram_tensor(name, list(shape), dt, kind="ExternalOutput").ap()

    def scratch(self, name, shape, dt):
        t = self.nc.dram_tensor(name, list(shape), dt).ap()
        self.dram[name] = t
        return t

    def bvec(self, name, li, n=1024):
        return bass.AP(self.vecs[name].tensor, li * n, [[0, 128], [1, n]])

    def declare(self):
        cfg = self.cfg
        D = cfg.depth
        self.x_in = [self.inp(f"x{s}", [L, D_MODEL], F32) for s, L in enumerate(cfg.seq_lens)]
        self.y_out = [self.outp(f"y{s}", [L, D_MODEL], F32) for s, L in enumerate(cfg.seq_lens)]
        self.wA = self.inp("wA", [D, 128, 8, WA_COLS], F32)
        self.bA_fm = self.inp("bA_fm", [D, 128, NFM], F32)
        self.bA_tm = self.inp("bA_tm", [D, 1, 768], F32)
        self.w_br = [self.inp(n, [D, 128, 4, 1024], F32) for n in ("w_branch_a", "w_branch_b", "w_branch_c")]
        self.w_out = self.inp("w_out", [D, 128, 8, 1024], F32)
        self.w_up = self.inp("w_up", [D, 128, 8, 4096], F32)
        self.w_down = self.inp("w_down", [D, 128, 32, 1024], F32)
        self.b_up = self.inp("b_up", [D, 128, 32], F32)
        self.vecs = {n: self.inp(n, [D, 1, 1024], F32) for n in ("b_down", "ln1_g", "ln1_b", "ln2_g", "ln2_b")}
        self.ident = self.inp("ident", [128, 128], BF16)
        self.na_tab = self.inp("na_tab", [D, 2, 128, 8, 8, 64], F32)
        self.na_mask = self.inp("na_mask", [128, 64], F32)
        self.swa_sink = self.inp("swa_sink", [D, 1, 8], F32)
        self.al_hi = self.inp("al_hi", [128, 8, 3, 128], BF16)
        self.al_lo = self.inp("al_lo", [128, 8, 3, 128], BF16)
        self.hy_w1 = self.inp("hy_w1", [D, 33, 64], F32)
        self.hy_w2 = self.inp("hy_w2", [D, 64, 64], F32)
        self.hy_w3 = self.inp("hy_w3", [D, 64, 2048], F32)
        self.hy_b1 = self.inp("hy_b1", [D, 64, 1], F32)
        self.hy_b2 = self.inp("hy_b2", [D, 64, 1], F32)
        self.hy_fr = self.inp("hy_fr", [D, 64, 1], F32)
        self.hy_bias = self.inp("hy_bias", [D, 1, 1024], F32)
        self.hy_cw = self.inp("hy_cw", [D, 128, 12, 3], F32)
        self.hy_cb = self.inp("hy_cb", [D, 128, 12], F32)
        self.absdelta = self.inp("absdelta", [128, 4], F32)
        self.C1 = self.inp("C1", [128, 256], BF16)
        self.C2 = self.inp("C2", [128, 256], BF16)
        self.HT = {}
        for L in cfg.ulens:
            N1 = 2 * L // 128
            NK1 = N1 // 2 + 1
            Na = N1 // 2
            d = {"N1": N1, "NK1": NK1, "Na": Na}
            d["zz"] = self.inp(f"zz{L}", [33, 2 * L], F32)
            d["tneg"] = self.inp(f"tneg{L}", [1, 2 * L], F32)
            d["F1"] = self.inp(f"F1_{L}", [N1, 2 * NK1], BF16)
            d["Mr"] = self.inp(f"Mr_{L}", [128, NK1, 128], BF16)
            d["Mi"] = self.inp(f"Mi_{L}", [128, NK1, 128], BF16)
            d["Tr"] = self.inp(f"Tr_{L}", [NK1, 128, Na], BF16)
            d["Tin"] = self.inp(f"Tin_{L}", [NK1, 128, Na], BF16)
            d["kT"] = self.scratch(f"kT{L}", [8, 128, 2 * L], BF16)
            d["H"] = self.scratch(f"H{L}", [128, 2, NK1, 2, 512], BF16)
            self.HT[L] = d
        self.S = []
        for s, L in enumerate(cfg.seq_lens):
            d = {}
            d["xmid"] = self.scratch(f"xmid{s}", [L, D_MODEL], F32)
            d["x1"] = self.scratch(f"x1_{s}", [L, D_MODEL], F32)
            d["qn"] = self.scratch(f"qn{s}", [4, 128, L], BF16)
            d["kn"] = self.scratch(f"kn{s}", [4, 128, L], BF16)
            d["vn"] = self.scratch(f"vn{s}", [L, 512], BF16)
            d["hy"] = self.scratch(f"hy{s}", [12, 128, L + 2], BF16)
            d["qs"] = self.scratch(f"qs{s}", [4, 128, L], BF16)
            d["ks"] = self.scratch(f"ks{s}", [2, 128, L], BF16)
            d["vs"] = self.scratch(f"vs{s}", [L, 256], BF16)
            d["g"] = self.scratch(f"g{s}", [24, 128, L], BF16)
            d["aT"] = self.scratch(f"aT{s}", [4, 128, L], BF16)
            d["zT"] = self.scratch(f"zT{s}", [4, 128, L], BF16)
            d["cT"] = self.scratch(f"cT{s}", [4, 128, L], BF16)
            d["uT"] = self.scratch(f"uT{s}", [12, 128, L], BF16)
            self.S.append(d)

    def phase_inproj(self, li, seqs):
        s = 'x'.join(str(q) for q in seqs)
        nc, cfg = self.nc, self.cfg
        bmap = [(q, t) for q in seqs for t in range(cfg.seq_lens[q] // 512)]
        NB = len(bmap)
        with ExitStack() as es:
            sb = lambda n, shp, dt: es.enter_context(nc.sbuf_tensor(f"A{li}{s}_{n}", shp, dt))
            ps = lambda n, shp, dt: es.enter_context(nc.psum_tensor(f"A{li}{s}_{n}", shp, dt))
            w = sb("w", [128, 8, WA_COLS], BF16)
            bfm = sb("bfm", [128, NFM], F32)
            btm32 = sb("btm32", [1, 768], F32)
            btm = sb("btm", [1, 768], BF16)
            ones = sb("ones", [1, 128], BF16)
            ident = sb("ident", [128, 128], BF16)
            zcol = sb("zcol", [128, 12, 1], BF16)
            xs = [sb(f"xs{i}", [128, 1024], F32) for i in range(4)]
            xb = [sb(f"xb{i}", [128, 1024], BF16) for i in range(4)]
            xT = [sb(f"xT{i}", [128, 8, 512], BF16) for i in range(2)]
            NST = 6
            st = [sb(f"st{i}", [128, 512], BF16) for i in range(NST)]
            pT = [ps(f"pT{i}", [128, 8, 128], BF16) for i in range(4)]
            NACC = 4
            acc = [ps(f"acc{i}", [128, 512], F32) for i in range(NACC)]
            sc = Sched(nc, f"A{li}{s}")
            for k in range(8):
                sc.dma("pool", w[:, k, :], self.wA[li, :, k, :], writes=[Part("w", k)])
            sc.dma("sp", bfm[:, :], self.bA_fm[li], writes=["bfm"])
            sc.dma("sp", btm32[:, :], self.bA_tm[li], writes=["btm32"])
            sc.dma("sp", ident[:, :], self.ident[:, :], writes=["ident"])
            sc.add("dve", lambda e: e.tensor_copy(out=btm[:, :], in_=btm32[:, :]), reads=["btm32"], writes=["btm"])
            sc.add("dve", lambda e: e.memset(ones[:, :], 1.0), writes=["ones"])
            sc.add("dve", lambda e: e.memset(zcol[:, :, :], 0.0), writes=["zcol"])
            for q in seqs:
                hy = self.S[q]["hy"]
                Lq = cfg.seq_lens[q]
                sc.dma("sp", hy[:, :, 0:1].rearrange("c p o -> p c o"), zcol[:, :, :], reads=["zcol"], allow_slow_non_contiguous=True)
                sc.dma("sp", hy[:, :, Lq + 1:Lq + 2].rearrange("c p o -> p c o"), zcol[:, :, :], reads=["zcol"], allow_slow_non_contiguous=True)
            sti = 0
            acci = 0
            def prep_load(tb):
                sq, ltb = bmap[tb]
                S = self.S[sq]
                x_src = self.x_in[sq] if li == 0 else S["xmid"]
                for t in range(4):
                    tok0 = ltb * 512 + t * 128
                    sc.dma("sp", xs[t][:, :], x_src[tok0:tok0 + 128, :], writes=[("xs", t)])
                    sc.add("pool", lambda e, o=xb[t], i=xs[t]: e.tensor_copy(out=o[:, :], in_=i[:, :]),
                           reads=[("xs", t)], writes=[("xb", t)])

            def prep_tr(tb):
                xTb = xT[tb % 2]
                kx = ("xT", tb % 2)
                for t in range(4):
                    def tr(e, o=pT[t], i=xb[t]):
                        last = None
                        for k in range(8):
                            last = e.transpose(out=o[:, k, :], in_=i[:, k * 128:(k + 1) * 128], identity=ident[:, :])
                        return last
                    sc.add("pe", tr, reads=[("xb", t), "ident"], writes=[("pT", t)])
                    sc.add("dve", lambda e, o=xTb, i=pT[t], t=t: e.tensor_copy(out=o[:, :, t * 128:(t + 1) * 128], in_=i[:, :, :]),
                           reads=[("pT", t)], writes=[Part(kx, t)])

            prep_load(0)
            prep_tr(0)
            for tb in range(NB):
                sq, ltb = bmap[tb]
                S = self.S[sq]
                xTb = xT[tb % 2]
                kx = ("xT", tb % 2)
                for ci, (nm, j, _, fn) in enumerate(FM_CHUNKS):
                    if ci == 0 and tb + 1 < NB:
                        prep_load(tb + 1)
                    if ci == NFM // 2 and tb + 1 < NB:
                        prep_tr(tb + 1)
                    a = acc[acci % NACC]
                    ka = ("acc", acci % NACC)
                    acci += 1

                    def mm(e, a=a, ci=ci, xTb=xTb):
                        last = None
                        for k in range(8):
                            last = e.matmul(a[:, :], lhsT=w[:, k, ci * 128:(ci + 1) * 128], rhs=xTb[:, k, :],
                                            start=(k == 0), stop=(k == 7))
                        return last
                    sc.add("pe", mm, reads=["w", kx], writes=[ka])
                    so = st[sti % NST]
                    ks = ("st", sti % NST)
                    sti += 1
                    func = AF.Sigmoid if fn == "sig" else AF.Identity
                    sc.add("act", lambda e, so=so, a=a, ci=ci, func=func: e.activation(out=so[:, :], in_=a[:, :], func=func,
                                                                                      bias=bfm[:, ci:ci + 1], scale=1.0),
                           reads=[ka, "bfm"], writes=[ks])
                    dst = S[nm]
                    off = 1 if nm == "hy" else 0
                    sc.dma("act", dst[j, :, off + ltb * 512: off + ltb * 512 + 512], so[:, :], reads=[ks])
                for t in range(4):
                    tok0 = ltb * 512 + t * 128
                    for gi, (nm, _) in enumerate(TM_GROUPS):
                        N = TM_N[gi]
                        c0 = TM_OFF[gi]
                        a = acc[acci % NACC]
                        ka = ("acc", acci % NACC)
                        acci += 1

                        def mm2(e, a=a, c0=c0, N=N, t=t, xTb=xTb, gi=gi):
                            for k in range(8):
                                e.matmul(a[:, 0:N], lhsT=xTb[:, k, t * 128:(t + 1) * 128], rhs=w[:, k, c0:c0 + N],
                                         start=(k == 0), stop=False)
                            b0 = 0 if gi == 0 else 512
                            return e.matmul(a[:, 0:N], lhsT=ones[:, :], rhs=btm[:, b0:b0 + N], start=False, stop=True)
                        sc.add("pe", mm2, reads=["w", kx, "ones", "btm"], writes=[ka])
                        so = st[sti % NST]
                        ks = ("st", sti % NST)
                        sti += 1
                        sc.add("dve", lambda e, so=so, a=a, N=N: e.tensor_copy(out=so[:, 0:N], in_=a[:, 0:N]),
                               reads=[ka], writes=[ks])
                        sc.dma("act", S[nm][tok0:tok0 + 128, :], so[:, 0:N], reads=[ks])
            sc.run()

    def _ln(self, sc, e_stats, src, srckey, dst, dstkey, gam, bet, tmp, uid):
        stats, mv, rstd = tmp
        k_st, k_mv, k_rs = (("lnst", uid), ("lnmv", uid), ("lnrs", uid))
        for h in range(2):
            sc.add("dve", lambda e, h=h: e.bn_stats(out=stats[:, h * 6:(h + 1) * 6], in_=src[:, h * 512:(h + 1) * 512]),
                   reads=[srckey], writes=[Part(k_st, h)])
        sc.add("dve", lambda e: e.bn_aggr(out=mv[:, :], in_=stats[:, :]), reads=[k_st], writes=[k_mv])
        sc.add("act", lambda e: e.activation(out=rstd[:, :], in_=mv[:, 1:2], func=AF.Sqrt, bias=self.epsc[:, 0:1], scale=1.0),
               reads=[k_mv, "epsc"], writes=[k_rs])
        sc.add("dve", lambda e: e.reciprocal(out=rstd[:, :], in_=rstd[:, :]), reads=[k_rs], writes=[k_rs])
        sc.add("dve", lambda e: e.scalar_tensor_tensor(out=src[:, :], in0=src[:, :], scalar=mv[:, 0:1], in1=gam[:, :],
                                                       op0=ALU.subtract, op1=ALU.mult), reads=[srckey, k_mv, "lnc"], writes=[srckey])
        sc.add("dve", lambda e: e.scalar_tensor_tensor(out=dst[:, :], in0=src[:, :], scalar=rstd[:, 0:1], in1=bet[:, :],
                                                       op0=ALU.mult, op1=ALU.add), reads=[srckey, k_rs, "lnc"], writes=[dstkey])

    def phase_merge(self, li, seqs):
        s = 'x'.join(str(q) for q in seqs)
        nc, cfg = self.nc, self.cfg
        bmap = [(q, t) for q in seqs for t in range(cfg.seq_lens[q] // 512)]
        NB = len(bmap)
        mixers = cfg.mixers
        srcs = [("na", "aT", 0), ("hy", "zT", 1), ("swa", "cT", 2)]
        act_src = [m for m in srcs if m[0] in mixers]
        with ExitStack() as es:
            sb = lambda n, shp, dt: es.enter_context(nc.sbuf_tensor(f"E{li}{s}_{n}", shp, dt))
            ps = lambda n, shp, dt: es.enter_context(nc.psum_tensor(f"E{li}{s}_{n}", shp, dt))
            wbr = [sb(f"wbr{i}", [128, 4, 1024], BF16) for i in range(3)]
            wo = sb("wo", [128, 8, 1024], BF16)
            gam = sb("gam", [128, 1024], F32)
            bet = sb("bet", [128, 1024], F32)
            br = [[sb(f"br{i}_{b}", [128, 4, 512], BF16) for b in range(2)] for i in range(3)]
            gt = [sb(f"gt{b}", [128, 24, 512], BF16) for b in range(2)]
            xs = [sb(f"xs{i}", [128, 1024], F32) for i in range(4)]
            mg = [sb(f"mg{b}", [128, 8, 512], BF16) for b in range(2)]
            tmpa = [sb(f"tmpa{i}", [128, 512], F32) for i in range(2)]
            sres = [sb(f"sres{i}", [128, 1024], F32) for i in range(2)]
            xo = [sb(f"xo{i}", [128, 1024], F32) for i in range(2)]
            lnt = [(sb(f"lst{i}", [128, 12], F32), sb(f"lmv{i}", [128, 2], F32), sb(f"lrs{i}", [128, 1], F32)) for i in range(2)]
            acc = [ps(f"acc{i}", [128, 512], F32) for i in range(6)]
            sc = Sched(nc, f"E{li}{s}")
            self.epsc = sb("epsc", [128, 1], F32)
            sc.add("dve", lambda e: e.memset(self.epsc[:, :], LN_EPS), writes=["epsc"])
            for i in range(3):
                for k in range(4):
                    sc.dma("pool", wbr[i][:, k, :], self.w_br[i][li, :, k, :], writes=[Part(("wbr", i), k)])
            for k in range(8):
                sc.dma("pool", wo[:, k, :], self.w_out[li, :, k, :], writes=[Part("wo", k)])
            sc.dma("sp", gam[:, :], self.bvec("ln1_g", li), writes=[Part("lnc", 0)])
            sc.dma("sp", bet[:, :], self.bvec("ln1_b", li), writes=[Part("lnc", 1)])
            acci = 0
            for tb in range(NB):
                sq, ltb = bmap[tb]
                S = self.S[sq]
                x_src = self.x_in[sq] if li == 0 else S["xmid"]
                b = tb % 2
                c0 = ltb * 512
                for (mn, dn, i) in act_src:
                    sc.dma("sp", br[i][b][:, :, :], S[dn][:, :, c0:c0 + 512].rearrange("c p t -> p c t"), writes=[("br", i, b)])
                if act_src:
                    sc.dma("sp", gt[b][:, :, :], S["g"][:, :, c0:c0 + 512].rearrange("c p t -> p c t"), writes=[("gt", b)])
                for oc in range(8):
                    first = True
                    for (mn, dn, i) in act_src:
                        a = acc[acci % 6]
                        ka = ("acc", acci % 6)
                        acci += 1

                        def mm(e, a=a, i=i, oc=oc, b=b):
                            last = None
                            for k in range(4):
                                last = e.matmul(a[:, :], lhsT=wbr[i][:, k, oc * 128:(oc + 1) * 128], rhs=br[i][b][:, k, :],
                                                start=(k == 0), stop=(k == 3))
                            return last
                        sc.add("pe", mm, reads=[("wbr", i), ("br", i, b)], writes=[ka])
                        gsl = gt[b][:, i * 8 + oc, :]
                        kt = ("tmpa", oc % 2)
                        tm = tmpa[oc % 2]
                        last_one = (mn == act_src[-1][0])
                        dst = mg[b][:, oc, :] if last_one else tm[:, :]
                        dkey = Part(("mg", b), oc) if last_one else kt
                        if first:
                            sc.add("dve", lambda e, dst=dst, a=a, gsl=gsl: e.tensor_tensor(out=dst, in0=a[:, :], in1=gsl, op=ALU.mult),
                                   reads=[ka, ("gt", b)], writes=[dkey])
                            first = False
                        else:
                            kt2 = ("tmpb", oc % 2)
                            sc.add("dve", lambda e, a=a, gsl=gsl, oc=oc: e.tensor_tensor(out=sres[oc % 2][:, 0:512], in0=a[:, :], in1=gsl, op=ALU.mult),
                                   reads=[ka, ("gt", b)], writes=[kt2])
                            sc.add("pool", lambda e, dst=dst, tm=tm, oc=oc: e.tensor_tensor(out=dst, in0=tm[:, :], in1=sres[oc % 2][:, 0:512], op=ALU.add),
                                   reads=[kt, kt2], writes=[dkey])
                for t in range(4):
                    gi = tb * 4 + t
                    sl = gi % 2
                    xl = gi % 4
                    tok0 = c0 + t * 128
                    sc.dma("sp", xs[xl][:, :], x_src[tok0:tok0 + 128, :], writes=[("xs", xl)])
                    if act_src:
                        for h in range(2):
                            a = acc[acci % 6]
                            ka = ("acc", acci % 6)
                            acci += 1

                            def mm3(e, a=a, h=h, t=t, b=b):
                                last = None
                                for k in range(8):
                                    last = e.matmul(a[:, :], lhsT=mg[b][:, k, t * 128:(t + 1) * 128], rhs=wo[:, k, h * 512:(h + 1) * 512],
                                                    start=(k == 0), stop=(k == 7))
                                return last
                            sc.add("pe", mm3, reads=[("mg", b), "wo"], writes=[ka])
                            sc.add("dve", lambda e, a=a, h=h, xl=xl: e.scalar_tensor_tensor(
                                out=xs[xl][:, h * 512:(h + 1) * 512], in0=xs[xl][:, h * 512:(h + 1) * 512], scalar=ALPHA,
                                in1=a[:, :], op0=ALU.mult, op1=ALU.add), reads=[ka, ("xs", xl)], writes=[("xs", xl)])
                    else:
                        sc.add("dve", lambda e, xl=xl: e.tensor_scalar(out=xs[xl][:, :], in0=xs[xl][:, :], scalar1=ALPHA, scalar2=None,
                                                                       op0=ALU.mult), reads=[("xs", xl)], writes=[("xs", xl)])
                    self._ln(sc, None, xs[xl], ("xs", xl), xo[sl], ("xo", sl), gam, bet, lnt[sl], sl)
                    sc.dma("act", S["x1"][tok0:tok0 + 128, :], xo[sl][:, :], reads=[("xo", sl)])
            sc.run()

    def phase_mlp(self, li, seqs):
        s = 'x'.join(str(q) for q in seqs)
        nc, cfg = self.nc, self.cfg
        TB = 256
        bmap = [(q, t) for q in seqs for t in range(cfg.seq_lens[q] // TB)]
        NB = len(bmap)
        with ExitStack() as es:
            sb = lambda n, shp, dt: es.enter_context(nc.sbuf_tensor(f"F{li}{s}_{n}", shp, dt))
            ps = lambda n, shp, dt: es.enter_context(nc.psum_tensor(f"F{li}{s}_{n}", shp, dt))
            wu = sb("wu", [128, 8, 4096], BF16)
            wd = sb("wd", [128, 32, 1024], BF16)
            bu = sb("bu", [128, 32], F32)
            gam = sb("gam", [128, 1024], F32)
            bet = sb("bet", [128, 1024], F32)
            bdn = sb("bdn", [128, 1024], F32)
            ident = sb("ident", [128, 128], BF16)
            xs = [sb(f"xs{i}", [128, 1024], F32) for i in range(3)]
            xb = [sb(f"xb{i}", [128, 1024], BF16) for i in range(2)]
            xT = [sb(f"xT{i}", [128, 8, TB], BF16) for i in range(2)]
            hT = [sb(f"hT{i}", [128, 32, TB], BF16) for i in range(2)]
            yt = [sb(f"yt{i}", [128, TB], F32) for i in range(2)]
            xo = [sb(f"xo{i}", [128, 1024], F32) for i in range(2)]
            lnt = [(sb(f"lst{i}", [128, 12], F32), sb(f"lmv{i}", [128, 2], F32), sb(f"lrs{i}", [128, 1], F32)) for i in range(2)]
            pT = [ps(f"pT{i}", [128, 8, 128], BF16) for i in range(2)]
            acc = [ps(f"acc{i}", [128, 512], F32) for i in range(6)]
            sc = Sched(nc, f"F{li}{s}")
            self.epsc = sb("epsc", [128, 1], F32)
            sc.add("dve", lambda e: e.memset(self.epsc[:, :], LN_EPS), writes=["epsc"])
            for k in range(8):
                sc.dma("pool", wu[:, k, :], self.w_up[li, :, k, :], writes=[Part("wu", k)])
            for k in range(32):
                sc.dma("pool", wd[:, k, :], self.w_down[li, :, k, :], writes=[Part("wd", k)])
            sc.dma("sp", bu[:, :], self.b_up[li], writes=["bu"])
            sc.dma("sp", ident[:, :], self.ident[:, :], writes=["ident"])
            sc.dma("sp", gam[:, :], self.bvec("ln2_g", li), writes=[Part("lnc", 0)])
            sc.dma("sp", bet[:, :], self.bvec("ln2_b", li), writes=[Part("lnc", 1)])
            sc.dma("sp", bdn[:, :], self.bvec("b_down", li), writes=["bdn"])
            acci = 0
            NT = TB // 128
            xkeep = {}
            for tb in range(NB):
                sq, ltb = bmap[tb]
                S = self.S[sq]
                dst_t = self.y_out[sq] if li == cfg.depth - 1 else S["xmid"]
                b = tb % 2
                kx = ("xT", b)
                for t in range(NT):
                    gi = tb * NT + t
                    sl = gi % 2
                    xl = gi % 3
                    tok0 = ltb * TB + t * 128
                    sc.dma("sp", xs[xl][:, :], S["x1"][tok0:tok0 + 128, :], writes=[("xs", xl)])
                    sc.add("pool", lambda e, o=xb[sl], i=xs[xl]: e.tensor_copy(out=o[:, :], in_=i[:, :]),
                           reads=[("xs", xl)], writes=[("xb", sl)])

                    def tr(e, o=pT[sl], i=xb[sl]):
                        last = None
                        for k in range(8):
                            last = e.transpose(out=o[:, k, :], in_=i[:, k * 128:(k + 1) * 128], identity=ident[:, :])
                        return last
                    sc.add("pe", tr, reads=[("xb", sl), "ident"], writes=[("pT", sl)])
                    sc.add("dve", lambda e, o=xT[b], i=pT[sl], t=t: e.tensor_copy(out=o[:, :, t * 128:(t + 1) * 128], in_=i[:, :, :]),
                           reads=[("pT", sl)], writes=[Part(kx, t)])
                    sc.add("dve", lambda e, xl=xl: e.scalar_tensor_tensor(out=xs[xl][:, :], in0=xs[xl][:, :], scalar=ALPHA, in1=bdn[:, :],
                                                                          op0=ALU.mult, op1=ALU.add),
                           reads=[("xs", xl), ("xb", sl), "bdn"], writes=[("xs", xl)])
                for fc in range(32):
                    a = acc[acci % 6]
                    ka = ("acc", acci % 6)
                    acci += 1

                    def mm(e, a=a, fc=fc, b=b):
                        last = None
                        for k in range(8):
                            last = e.matmul(a[:, 0:TB], lhsT=wu[:, k, fc * 128:(fc + 1) * 128], rhs=xT[b][:, k, :],
                                            start=(k == 0), stop=(k == 7))
                        return last
                    sc.add("pe", mm, reads=["wu", kx], writes=[ka])
                    y = yt[fc % 2]
                    ky = ("yt", fc % 2)
                    sc.add("act", lambda e, y=y, a=a, fc=fc: e.activation(out=y[:, :], in_=a[:, 0:TB], func=AF.Relu,
                                                                          bias=bu[:, fc:fc + 1], scale=1.0),
                           reads=[ka, "bu"], writes=[ky])
                    eng = "dve" if fc % 2 == 0 else "pool"
                    sc.add(eng, lambda e, y=y, fc=fc, b=b: e.tensor_tensor(out=hT[b][:, fc, :], in0=y[:, :], in1=y[:, :], op=ALU.mult),
                           reads=[ky], writes=[Part(("hT", b), fc)])
                for t in range(NT):
                    gi = tb * NT + t
                    sl = gi % 2
                    xl = gi % 3
                    tok0 = ltb * TB + t * 128
                    for h in range(2):
                        a = acc[acci % 6]
                        ka = ("acc", acci % 6)
                        acci += 1

                        def mm3(e, a=a, h=h, t=t, b=b):
                            last = None
                            for k in range(32):
                                last = e.matmul(a[:, :], lhsT=hT[b][:, k, t * 128:(t + 1) * 128], rhs=wd[:, k, h * 512:(h + 1) * 512],
                                                start=(k == 0), stop=(k == 31))
                            return last
                        sc.add("pe", mm3, reads=[("hT", b), "wd"], writes=[ka])
                        sc.add("dve", lambda e, a=a, h=h, xl=xl: e.tensor_tensor(
                            out=xs[xl][:, h * 512:(h + 1) * 512], in0=xs[xl][:, h * 512:(h + 1) * 512], in1=a[:, :], op=ALU.add),
                            reads=[ka, ("xs", xl)], writes=[("xs", xl)])
                    self._ln(sc, None, xs[xl], ("xs", xl), xo[sl], ("xo", sl), gam, bet, lnt[sl], sl)
                    sc.dma("act", dst_t[tok0:tok0 + 128, :], xo[sl][:, :], reads=[("xo", sl)])
            sc.run()

    def phase_na(self, li, s):
        nc, cfg = self.nc, self.cfg
        L = cfg.seq_lens[s]
        S = self.S[s]
        rows = L // GRID_W
        with ExitStack() as es:
            sb = lambda n, shp, dt: es.enter_context(nc.sbuf_tensor(f"B{li}{s}_{n}", shp, dt))
            ps = lambda n, shp, dt: es.enter_context(nc.psum_tensor(f"B{li}{s}_{n}", shp, dt))
            ident = sb("ident", [128, 128], BF16)
            ones = sb("ones", [128, 128], BF16)
            t32 = sb("t32", [128, 8 * 8 * 64], F32)
            mk = sb("mk", [128, 64], F32)
            tt = [sb(f"tt{t}", [128, 8, 8, 64], BF16) for t in range(2)]
            qT = [sb(f"qT{i}", [128, 4, 512], BF16) for i in range(2)]
            kT = [sb(f"kT{i}", [128, 4, 512], BF16) for i in range(2)]
            vw = [sb(f"vw{i}", [128, 4, 512], BF16) for i in range(2)]
            P = [[sb(f"P{i}_{e}", [128, 2, 4, 64], BF16) for e in range(2)] for i in range(2)]
            rec = [sb(f"rec{i}", [128, 256], F32) for i in range(2)]
            ao = [sb(f"ao{i}", [128, 4, 512], BF16) for i in range(2)]
            Sp = [[ps(f"S{i}_{e}", [128, 2, 4, 64], F32) for e in range(2)] for i in range(2)]
            Op = [ps(f"O{i}", [128, 512], F32) for i in range(2)]
            sc = Sched(nc, f"B{li}{s}")
            sc.dma("sp", ident[:, :], self.ident[:, :], writes=["ident"])
            sc.dma("sp", mk[:, :], self.na_mask[:, :], writes=["mk"])
            sc.add("dve", lambda e: e.memset(ones[:, :], 1.0), writes=["ones"])
            for t in range(2):
                sc.dma("sp", t32[:, :], self.na_tab[li, t].rearrange("p h m w -> p (h m w)"), writes=["t32"])
                sc.add("dve", lambda e, t=t: e.scalar_tensor_tensor(
                    out=sbap(tt[t], 0, 128, 0, [[64, 64], [1, 64]]), in0=sbap(t32, 0, 128, 0, [[64, 64], [1, 64]]),
                    scalar=float(HEAD_DIM ** 0.5), in1=sbap(mk, 0, 128, 0, [[0, 64], [1, 64]]), op0=ALU.mult, op1=ALU.add),
                    reads=["t32", "mk"], writes=[("tt", t)])
            def loads(r):
                rs = min(max(r - NA_WIN_ROWS // 2, 0), rows - NA_WIN_ROWS)
                r8, rr = divmod(r, 8)
                qb = r8 % 2
                if rr == 0:
                    sc.dma("sp", qT[qb][:, :, :], S["qn"][:, :, r * 64:r * 64 + 512].rearrange("c p t -> p c t"), writes=[("qT", qb)])
                wb = r % 2
                sc.dma("sp", kT[wb][:, :, :], S["kn"][:, :, rs * 64:rs * 64 + 512].rearrange("c p t -> p c t"), writes=[("kT", wb)])
                sc.dma("sp", vw[wb][:, :, :], S["vn"][rs * 64:rs * 64 + 512, :].rearrange("(j p) f -> p j f", p=128), writes=[("vw", wb)])

            def s1(it):
                r, hq = divmod(it, 2)
                if hq == 0:
                    loads(r)
                rs = min(max(r - NA_WIN_ROWS // 2, 0), rows - NA_WIN_ROWS)
                dr = r - rs
                r8, rr = divmod(r, 8)
                qb = r8 % 2
                wb = r % 2
                if dr % 2 == 1:
                    tsel, m0 = 0, (7 - dr) // 2
                else:
                    tsel, m0 = 1, (6 - dr) // 2
                sb_i = it % 2
                Sb, Pb = Sp[sb_i], P[sb_i]

                def qk(e):
                    for ee in range(2):
                        for hpi in range(2):
                            h = 2 * (2 * hq + hpi) + ee
                            e.matmul(Sb[ee][:, hpi, :, :], lhsT=ident[:, :], rhs=tt[tsel][:, h, m0:m0 + 4, :], start=(hpi == 0), stop=False)
                    last = None
                    for ee in range(2):
                        for hpi in range(2):
                            hp = 2 * hq + hpi
                            for j in range(4):
                                last = e.matmul(Sb[ee][:, hpi, j, :], lhsT=kT[wb][64 * ee:64 * ee + 64, hp, j * 128:(j + 1) * 128],
                                                rhs=qT[qb][64 * ee:64 * ee + 64, hp, rr * 64:(rr + 1) * 64],
                                                start=False, stop=(hpi == 1 and j == 3))
                    return last
                sc.add("pe", qk, reads=["ident", ("tt", tsel), ("kT", wb), ("qT", qb)], writes=[("S", sb_i)])
                for ee in range(2):
                    sc.add("act", lambda e, ee=ee: e.activation(out=Pb[ee][:, :, :, :], in_=Sb[ee][:, :, :, :], func=AF.Exp,
                                                                 scale=float(HEAD_DIM ** -0.5)),
                           reads=[("S", sb_i)], writes=[Part(("P", sb_i), ee)])

            def s2(it):
                r, hq = divmod(it, 2)
                r8, rr = divmod(r, 8)
                wb = r % 2
                ab = r8 % 2
                sb_i = it % 2
                Pb, Ob, rb = P[sb_i], Op[sb_i], rec[sb_i]

                def pv(e):
                    for hpi in range(2):
                        hp = 2 * hq + hpi
                        for ee in range(2):
                            o0 = (hpi * 2 + ee) * 64
                            for j in range(4):
                                e.matmul(Ob[:, o0:o0 + 64], lhsT=vw[wb][:, j, hp * 128:(hp + 1) * 128], rhs=Pb[ee][:, hpi, j, :],
                                         start=(j == 0), stop=(j == 3))
                    last = None
                    for ee in range(2):
                        d0 = 256 + ee * 128
                        for j in range(4):
                            last = e.matmul(Ob[:, d0:d0 + 128], lhsT=ones[:, :], rhs=Pb[ee][:, :, j, :], start=(j == 0), stop=(j == 3))
                    return last
                sc.add("pe", pv, reads=[("P", sb_i), ("vw", wb), "ones"], writes=[("O", sb_i)])
                sc.add("dve", lambda e: e.reciprocal(out=rb[:, :], in_=Ob[:, 256:512]), reads=[("O", sb_i)], writes=[("rec", sb_i)])
                for hpi in range(2):
                    hp = 2 * hq + hpi
                    for ee in range(2):
                        o0 = (hpi * 2 + ee) * 64
                        d0 = (ee * 2 + hpi) * 64
                        sc.add("dve", lambda e, ee=ee, hp=hp, o0=o0, d0=d0: e.tensor_tensor(
                            out=ao[ab][64 * ee:64 * ee + 64, hp, rr * 64:(rr + 1) * 64], in0=Ob[64 * ee:64 * ee + 64, o0:o0 + 64],
                            in1=rb[64 * ee:64 * ee + 64, d0:d0 + 64], op=ALU.mult),
                            reads=[("O", sb_i), ("rec", sb_i)], writes=[Part(("ao", ab), (hp, rr, ee))])
                if rr == 7 and hq == 1:
                    sc.dma("act", S["aT"][:, :, (r - 7) * 64:(r - 7) * 64 + 512].rearrange("c p t -> p c t"), ao[ab][:, :, :], reads=[("ao", ab)])

            NIT = rows * 2
            s1(0)
            for it in range(NIT):
                if it + 1 < NIT:
                    s1(it + 1)
                s2(it)
            sc.run()

    def phase_swa(self, li, s):
        nc, cfg = self.nc, self.cfg
        L = cfg.seq_lens[s]
        S = self.S[s]
        NBq = L // 128
        with ExitStack() as es:
            sb = lambda n, shp, dt: es.enter_context(nc.sbuf_tensor(f"C{li}{s}_{n}", shp, dt))
            ps = lambda n, shp, dt: es.enter_context(nc.psum_tensor(f"C{li}{s}_{n}", shp, dt))
            ident = sb("ident", [128, 128], BF16)
            ones = sb("ones", [128, 128], BF16)
            alh = sb("alh", [128, 8, 3, 128], BF16)
            all_ = sb("all", [128, 8, 3, 128], BF16)
            snk = sb("snk", [128, 8], F32)
            esk = sb("esk", [128, 8], F32)
            qT = [sb(f"qT{i}", [128, 4, 512], BF16) for i in range(2)]
            kT = [sb(f"kT{i}", [128, 2, 384], BF16) for i in range(2)]
            vw = [sb(f"vw{i}", [128, 3, 256], BF16) for i in range(2)]
            P = [sb(f"P{i}", [128, 2, 3, 128], BF16) for i in range(2)]
            rec = [sb(f"rec{i}", [128, 256], F32) for i in range(2)]
            co = [sb(f"co{i}", [128, 4, 512], BF16) for i in range(2)]
            Sp = [[ps(f"S{i}_{e}", [128, 4, 128], F32) for e in range(2)] for i in range(2)]
            Op = [ps(f"O{i}", [128, 512], F32) for i in range(2)]
            sc = Sched(nc, f"C{li}{s}")
            sc.dma("sp", ident[:, :], self.ident[:, :], writes=["ident"])
            sc.dma("sp", alh[:, :, :, :], self.al_hi[:, :, :, :], writes=["alh"])
            sc.dma("sp", all_[:, :, :, :], self.al_lo[:, :, :, :], writes=["all"])
            sc.dma("sp", snk[:, :], bass.AP(self.swa_sink.tensor, li * 8, [[0, 128], [1, 8]]), writes=["snk"])
            sc.add("act", lambda e: e.activation(out=esk[:, :], in_=snk[:, :], func=AF.Exp), reads=["snk"], writes=["esk"])
            sc.add("dve", lambda e: e.memset(ones[:, :], 1.0), writes=["ones"])
            def geom(bi):
                clo = 0 if bi > 0 else 1
                chi = 3 if bi < NBq - 1 else 2
                return clo, chi

            def loads(bi):
                b4, bb = divmod(bi, 4)
                qb = b4 % 2
                if bb == 0:
                    sc.dma("sp", qT[qb][:, :, :], S["qs"][:, :, bi * 128:bi * 128 + 512].rearrange("c p t -> p c t"), writes=[("qT", qb)])
                wb = bi % 2
                clo, chi = geom(bi)
                t0 = (bi - 1 + clo) * 128
                nt = (chi - clo) * 128
                sc.dma("sp", kT[wb][:, :, clo * 128:chi * 128], S["ks"][:, :, t0:t0 + nt].rearrange("c p t -> p c t"), writes=[("kT", wb)])
                sc.dma("sp", vw[wb][:, clo:chi, :], S["vs"][t0:t0 + nt, :].rearrange("(j p) f -> p j f", p=128), writes=[("vw", wb)])

            def s1(it):
                bi, hp = divmod(it, 4)
                if hp == 0:
                    loads(bi)
                b4, bb = divmod(bi, 4)
                qb = b4 % 2
                wb = bi % 2
                clo, chi = geom(bi)
                g = hp // 2
                sb_i = it % 2
                Pb = P[sb_i]
                for ee in range(2):
                    h = 2 * hp + ee
                    Sb = Sp[sb_i][ee]

                    def qk(e, Sb=Sb, h=h, ee=ee):
                        e.matmul(Sb[:, clo:chi, :], lhsT=ident[:, :], rhs=alh[:, h, clo:chi, :], start=True, stop=False)
                        e.matmul(Sb[:, clo:chi, :], lhsT=ident[:, :], rhs=all_[:, h, clo:chi, :], start=False, stop=False)
                        last = None
                        for c in range(clo, chi):
                            last = e.matmul(Sb[:, c, :], lhsT=kT[wb][64 * ee:64 * ee + 64, g, c * 128:(c + 1) * 128],
                                            rhs=qT[qb][64 * ee:64 * ee + 64, hp, bb * 128:(bb + 1) * 128],
                                            start=False, stop=(c == chi - 1))
                        return last
                    sc.add("pe", qk, reads=["ident", "alh", "all", ("kT", wb), ("qT", qb)], writes=[("S", sb_i, ee)])
                    sc.add("act", lambda e, Sb=Sb, ee=ee: e.activation(
                        out=Pb[:, ee, clo:chi, :], in_=Sb[:, clo:chi, :], func=AF.Exp, scale=float(HEAD_DIM ** -0.5)),
                        reads=[("S", sb_i, ee)], writes=[Part(("P", sb_i), ee)])

            def s2(it):
                bi, hp = divmod(it, 4)
                b4, bb = divmod(bi, 4)
                wb = bi % 2
                clo, chi = geom(bi)
                g = hp // 2
                sb_i = it % 2
                Pb, Ob, rb = P[sb_i], Op[sb_i], rec[sb_i]
                ab = b4 % 2

                def pv(e):
                    for ee in range(2):
                        for c in range(clo, chi):
                            e.matmul(Ob[:, ee * 128:(ee + 1) * 128], lhsT=vw[wb][:, c, g * 128:(g + 1) * 128], rhs=Pb[:, ee, c, :],
                                     start=(c == clo), stop=(c == chi - 1))
                    last = None
                    for c in range(clo, chi):
                        last = e.matmul(Ob[:, 256:512], lhsT=ones[:, :], rhs=Pb[:, :, c, :], start=(c == clo), stop=(c == chi - 1))
                    return last
                sc.add("pe", pv, reads=[("P", sb_i), ("vw", wb), "ones"], writes=[("O", sb_i)])
                for ee in range(2):
                    h = 2 * hp + ee
                    sc.add("dve", lambda e, ee=ee, h=h: e.tensor_scalar(
                        out=rb[:, ee * 128:(ee + 1) * 128], in0=Ob[:, 256 + ee * 128:256 + (ee + 1) * 128], scalar1=esk[:, h:h + 1], scalar2=None,
                        op0=ALU.add), reads=[("O", sb_i), "esk"], writes=[Part(("rec", sb_i), ee)])
                sc.add("dve", lambda e: e.reciprocal(out=rb[:, :], in_=rb[:, :]), reads=[("rec", sb_i)], writes=[("rec", sb_i)])
                for ee in range(2):
                    sc.add("dve", lambda e, ee=ee: e.tensor_tensor(
                        out=co[ab][64 * ee:64 * ee + 64, hp, bb * 128:(bb + 1) * 128], in0=Ob[64 * ee:64 * ee + 64, ee * 128:(ee + 1) * 128],
                        in1=rb[64 * ee:64 * ee + 64, ee * 128:(ee + 1) * 128], op=ALU.mult),
                        reads=[("O", sb_i), ("rec", sb_i)], writes=[Part(("co", ab), (hp, bb, ee))])
                if bb == 3 and hp == 3:
                    sc.dma("act", S["cT"][:, :, (bi - 3) * 128:(bi - 3) * 128 + 512].rearrange("c p t -> p c t"), co[ab][:, :, :], reads=[("co", ab)])

            NIT = NBq * 4
            s1(0)
            for it in range(NIT):
                if it + 1 < NIT:
                    s1(it + 1)
                s2(it)
            sc.run()

    @staticmethod
    def hy_group(L):
        return 32 if L >= 8192 else 64

    def declare_hy_scratch(self):
        for L in self.cfg.ulens:
            d = self.HT[L]
            G = self.hy_group(L)
            d["G"] = G
            d["Hh"] = self.scratch(f"Hh{L}", [2, 512 // G, 128, d["NK1"], 2, G], BF16)
            d["Hs"] = self.scratch(f"Hs{L}", [2, 512 // G, 128, d["NK1"], 2, G], BF16)

    def phase_filt(self, li, L):
        nc = self.nc
        d = self.HT[L]
        TWO_PI = 2.0 * math.pi
        with ExitStack() as es:
            sb = lambda n, shp, dt: es.enter_context(nc.sbuf_tensor(f"FA{li}_{L}_{n}", shp, dt))
            ps = lambda n, shp, dt: es.enter_context(nc.psum_tensor(f"FA{li}_{L}_{n}", shp, dt))
            w1 = sb("w1", [33, 64], F32)
            w2 = sb("w2", [64, 64], F32)
            w3 = sb("w3", [64, 2048], BF16)
            b1 = sb("b1", [64, 1], F32)
            b2 = sb("b2", [64, 1], F32)
            fr = sb("fr", [64, 1], F32)
            fb1 = sb("fb1", [64, 1], F32)
            fb2 = sb("fb2", [64, 1], F32)
            negpi = sb("negpi", [64, 1], F32)
            adl = sb("adl", [128, 4], F32)
            zzb = [sb(f"zzb{i}", [33, 512], F32) for i in range(2)]
            tnb = [sb(f"tnb{i}", [128, 512], F32) for i in range(2)]
            a1 = [sb(f"a1_{i}", [64, 512], F32) for i in range(2)]
            t1 = [sb(f"t1_{i}", [64, 512], F32) for i in range(2)]
            t2 = [sb(f"t2_{i}", [64, 512], F32) for i in range(2)]
            h1 = [sb(f"h1_{i}", [64, 512], F32) for i in range(2)]
            h2 = [sb(f"h2_{i}", [64, 512], BF16) for i in range(2)]
            win = [[sb(f"win{i}_{c}", [128, 512], F32) for c in range(4)] for i in range(2)]
            stg = [sb(f"stg{i}", [128, 512], BF16) for i in range(4)]
            pm = [ps(f"pm{i}", [128, 512], F32) for i in range(2)]
            p3 = [ps(f"p3_{i}", [128, 512], F32) for i in range(4)]
            sc = Sched(nc, f"FA{li}_{L}")
            sc.dma("sp", w1[:, :], self.hy_w1[li], writes=["w1"])
            sc.dma("sp", w2[:, :], self.hy_w2[li], writes=["w2"])
            sc.dma("pool", w3[:, :], self.hy_w3[li], writes=["w3"])
            sc.dma("sp", b1[:, :], self.hy_b1[li], writes=["b1"])
            sc.dma("sp", b2[:, :], self.hy_b2[li], writes=["b2"])
            sc.dma("sp", fr[:, :], self.hy_fr[li], writes=["fr"])
            sc.dma("sp", adl[:, :], self.absdelta[:, :], writes=["adl"])
            sc.add("dve", lambda e: e.memset(negpi[:, :], -math.pi), writes=["negpi"])
            for (fb, bb, kf, kb) in ((fb1, b1, "fb1", "b1"), (fb2, b2, "fb2", "b2")):
                sc.add("dve", lambda e, fb=fb, bb=bb: e.tensor_tensor(out=fb[:, :], in0=fr[:, :], in1=bb[:, :], op=ALU.mult),
                       reads=["fr", kb], writes=[kf])
            NBp = 2 * L // 512
            si = 0
            pi3 = 0
            for nb in range(NBp):
                i = nb % 2
                n0 = nb * 512
                sc.dma("sp", zzb[i][:, :], d["zz"][:, n0:n0 + 512], writes=[("zzb", i)])
                sc.dma("sp", tnb[i][:, :], bass.AP(d["tneg"].tensor, n0, [[0, 128], [1, 512]]), writes=[("tnb", i)])
                sc.add("pe", lambda e, i=i: e.matmul(pm[0][0:64, :], lhsT=w1[:, :], rhs=zzb[i][:, :], start=True, stop=True),
                       reads=["w1", ("zzb", i)], writes=[("pm", 0)])
                for (stage, pmi, fb, kf, src_h, dst_h, kd) in ((0, 0, fb1, "fb1", None, h1, "h1"), (1, 1, fb2, "fb2", h1, h2, "h2")):
                    if stage == 1:
                        sc.add("pe", lambda e, i=i: e.matmul(pm[1][0:64, :], lhsT=w2[:, :], rhs=h1[i][:, :], start=True, stop=True),
                               reads=["w2", ("h1", i)], writes=[("pm", 1)])
                    sc.add("dve", lambda e, i=i, pmi=pmi, fb=fb: e.tensor_scalar(out=a1[i][:, :], in0=pm[pmi][0:64, :], scalar1=fr[:, 0:1], scalar2=fb[:, 0:1],
                                                                                op0=ALU.mult, op1=ALU.add),
                           reads=[("pm", pmi), "fr", kf], writes=[("a1", i)])
                    sc.add("dve", lambda e, i=i: e.tensor_scalar(out=t1[i][:, :], in0=a1[i][:, :], scalar1=math.pi, scalar2=-TWO_PI, op0=ALU.is_gt, op1=ALU.mult),
                           reads=[("a1", i)], writes=[("t1", i)])
                    sc.add("dve", lambda e, i=i: e.tensor_scalar(out=t2[i][:, :], in0=a1[i][:, :], scalar1=-math.pi, scalar2=TWO_PI, op0=ALU.is_lt, op1=ALU.mult),
                           reads=[("a1", i)], writes=[("t2", i)])
                    sc.add("dve", lambda e, i=i: e.tensor_tensor(out=a1[i][:, :], in0=a1[i][:, :], in1=t1[i][:, :], op=ALU.add),
                           reads=[("a1", i), ("t1", i)], writes=[("a1", i)])
                    sc.add("dve", lambda e, i=i: e.tensor_tensor(out=a1[i][:, :], in0=a1[i][:, :], in1=t2[i][:, :], op=ALU.add),
                           reads=[("a1", i), ("t2", i)], writes=[("a1", i)])
                    sc.add("act", lambda e, i=i, dst_h=dst_h: e.activation(out=dst_h[i][:, :], in_=a1[i][:, :], func=AF.Sin),
                           reads=[("a1", i)], writes=[(kd, i)])
                dirn = 0 if n0 < L else 1
                for cc in range(4):
                    sc.add("act", lambda e, i=i, cc=cc: e.activation(out=win[i][cc][:, :], in_=tnb[i][:, :], func=AF.Exp, scale=adl[:, cc:cc + 1]),
                           reads=[("tnb", i), "adl"], writes=[("win", i, cc)])
                for o in range(2):
                    for cc in range(4):
                        pb = pi3 % 4
                        pi3 += 1
                        col = (o * 2 + dirn) * 512 + cc * 128
                        sc.add("pe", lambda e, pb=pb, col=col, i=i: e.matmul(p3[pb][:, :], lhsT=w3[:, col:col + 128], rhs=h2[i][:, :], start=True, stop=True),
                               reads=["w3", ("h2", i)], writes=[("p3", pb)])
                        sb_i = si % 4
                        si += 1
                        sc.add("dve", lambda e, pb=pb, sb_i=sb_i, i=i, cc=cc: e.tensor_tensor(out=stg[sb_i][:, :], in0=p3[pb][:, :], in1=win[i][cc][:, :], op=ALU.mult),
                               reads=[("p3", pb), ("win", i, cc)], writes=[("stg", sb_i)])
                        sc.dma("sp", d["kT"][o * 4 + cc, :, n0:n0 + 512], stg[sb_i][:, :], reads=[("stg", sb_i)])
            sc.run()

    def _fft_consts(self, sc, sb, L):
        d = self.HT[L]
        N1, NK1, Na = d["N1"], d["NK1"], d["Na"]
        env = dict(d)
        env["F1t"] = sb("F1t", [N1, 2 * NK1], BF16)
        env["Mrt"] = sb("Mrt", [128, NK1, 128], BF16)
        env["Mit"] = sb("Mit", [128, NK1, 128], BF16)
        sc.dma("sp", env["F1t"][:, :], d["F1"][:, :], writes=["F1t"])
        sc.dma("sp", env["Mrt"][:, :, :], d["Mr"][:, :, :], writes=["Mrt"])
        sc.dma("sp", env["Mit"][:, :, :], d["Mi"][:, :, :], writes=["Mit"])
        return env

    def _fft_fwd(self, sc, env, X, xkey, Kp, consume, part=0):
        G, NK1 = env["G"], env["NK1"]
        W2 = 2 * NK1
        A, A2 = env["A"], env["A2"]
        AK, A2K = env.get("akey", "A"), env.get("a2key", "A2")
        F1t, Mrt, Mit = env["F1t"], env["Mrt"], env["Mit"]
        CB = max(1, 512 // W2)
        ci = 0
        for c0 in (range(0, G, CB) if part in (0, 1) else ()):
            ncb = min(CB, G - c0)
            bi = env["psA_i"][0] % 2
            env["psA_i"][0] += 1
            bank = env["psA"][bi]
            kb = ("psA", bi)

            def mm(e, bank=bank, c0=c0, ncb=ncb):
                last = None
                for j in range(ncb):
                    last = e.matmul(bank[:, j * W2:(j + 1) * W2], lhsT=X[0:Kp, c0 + j, :], rhs=F1t[0:Kp, :], start=True, stop=True)
                return last
            sc.add("pe", mm, reads=[xkey, "F1t"], writes=[kb])
            o1 = sbap(A, 0, 128, c0, [[1, ncb], [G, W2]])
            i1 = sbap(bank, 0, 128, 0, [[W2, ncb], [1, W2]])
            o2 = sbap(A2, 0, 128, c0, [[1, ncb], [2 * G, NK1]])
            i2 = sbap(bank, 0, 128, 1, [[W2, ncb], [2, NK1]])
            o3 = sbap(A2, 0, 128, c0 + G, [[1, ncb], [2 * G, NK1]])
            i3 = sbap(bank, 0, 128, 0, [[W2, ncb], [2, NK1]])
            if bi == 0:
                sc.add("act", lambda e, o1=o1, i1=i1: e.activation(out=o1, in_=i1, func=AF.Identity), reads=[kb], writes=[Part(AK, c0)])
                sc.add("act", lambda e, o2=o2, i2=i2: e.activation(out=o2, in_=i2, func=AF.Identity, scale=-1.0), reads=[kb], writes=[Part(A2K, (c0, 0))])
                sc.add("act", lambda e, o3=o3, i3=i3: e.activation(out=o3, in_=i3, func=AF.Identity), reads=[kb], writes=[Part(A2K, (c0, 1))])
            else:
                sc.add("dve", lambda e, o1=o1, i1=i1: e.tensor_copy(out=o1, in_=i1), reads=[kb], writes=[Part(AK, c0)])
                sc.add("dve", lambda e, o2=o2, i2=i2: e.tensor_scalar(out=o2, in0=i2, scalar1=-1.0, scalar2=None, op0=ALU.mult),
                       reads=[kb], writes=[Part(A2K, (c0, 0))])
                sc.add("dve", lambda e, o3=o3, i3=i3: e.tensor_copy(out=o3, in_=i3), reads=[kb], writes=[Part(A2K, (c0, 1))])
        KB = 512 // (2 * G)
        for k0 in (range(0, NK1, KB) if part in (0, 2) else ()):
            nk = min(KB, NK1 - k0)
            bi = env["psX_i"][0] % 2
            env["psX_i"][0] += 1
            bank = env["psX"][bi]
            kb = ("psX", bi)

            def mm2(e, bank=bank, k0=k0, nk=nk):
                last = None
                for kk in range(nk):
                    k1 = k0 + kk
                    e.matmul(bank[:, kk * 2 * G:(kk + 1) * 2 * G], lhsT=Mrt[:, k1, :], rhs=A[:, k1, :, :], start=True, stop=False)
                    last = e.matmul(bank[:, kk * 2 * G:(kk + 1) * 2 * G], lhsT=Mit[:, k1, :], rhs=A2[:, k1, :, :], start=False, stop=True)
                return last
            sc.add("pe", mm2, reads=[AK, A2K, "Mrt", "Mit"], writes=[kb])
            consume(k0, nk, bank, kb)

    def _fft_inv(self, sc, env, Y, ykey, out_cb, part=0):
        G, NK1, Na = env["G"], env["NK1"], env["Na"]
        Bp = env["Bp"]
        C1t, C2t, Trt, Tint = env["C1t"], env["C2t"], env["Trt"], env["Tint"]
        for cb in (range(G // 2) if part in (0, 1) else ()):
            c0 = 2 * cb
            bi = env["psB_i"][0] % 2
            env["psB_i"][0] += 1
            bank = env["psB"][bi]
            kb = ("psB", bi)

            def mm(e, bank=bank, c0=c0):
                last = None
                for j in range(2):
                    e.matmul(bank[0:NK1, j * 256:(j + 1) * 256], lhsT=Y[:, :, 0, c0 + j], rhs=C1t[:, :], start=True, stop=False)
                    last = e.matmul(bank[0:NK1, j * 256:(j + 1) * 256], lhsT=Y[:, :, 1, c0 + j], rhs=C2t[:, :], start=False, stop=True)
                return last
            sc.add("pe", mm, reads=[ykey, "C1t", "C2t"], writes=[kb])
            eng = "act" if bi == 0 else "dve"
            oap = sbap(Bp, 0, NK1, c0, [[1, 2], [G, 2], [2 * G, 128]])
            iap = sbap(bank, 0, NK1, 0, [[256, 2], [128, 2], [1, 128]])
            if eng == "act":
                sc.add("act", lambda e, oap=oap, iap=iap: e.activation(out=oap, in_=iap, func=AF.Identity), reads=[kb], writes=[Part("Bp", c0)])
            else:
                sc.add("dve", lambda e, oap=oap, iap=iap: e.tensor_copy(out=oap, in_=iap), reads=[kb], writes=[Part("Bp", c0)])
        BB = 512 // G
        for b0 in (range(0, 128, BB) if part in (0, 2) else ()):
            bi = env["psY_i"][0] % 2
            env["psY_i"][0] += 1
            bank = env["psY"][bi]
            kb = ("psY", bi)

            def mm2(e, bank=bank, b0=b0):
                last = None
                for bb in range(BB):
                    b = b0 + bb
                    e.matmul(bank[0:Na, bb * G:(bb + 1) * G], lhsT=Trt[0:NK1, b, :], rhs=Bp[0:NK1, b, 0, :], start=True, stop=False)
                    last = e.matmul(bank[0:Na, bb * G:(bb + 1) * G], lhsT=Tint[0:NK1, b, :], rhs=Bp[0:NK1, b, 1, :], start=False, stop=True)
                return last
            sc.add("pe", mm2, reads=["Bp", "Trt", "Tint"], writes=[kb])
            out_cb(b0, BB, bank, kb)

    def phase_fspec(self, li, L):
        nc = self.nc
        d = self.HT[L]
        G = d["G"]
        N1, NK1 = d["N1"], d["NK1"]
        NG = 512 // G
        KB = 512 // (2 * G)
        with ExitStack() as es:
            sb = lambda n, shp, dt: es.enter_context(nc.sbuf_tensor(f"FS{li}_{L}_{n}", shp, dt))
            ps = lambda n, shp, dt: es.enter_context(nc.psum_tensor(f"FS{li}_{L}_{n}", shp, dt))
            sc = Sched(nc, f"FS{li}_{L}")
            env = self._fft_consts(sc, sb, L)
            Ab = [sb(f"A_{i}", [128, NK1, 2, G], BF16) for i in range(2)]
            A2b = [sb(f"A2_{i}", [128, NK1, 2, G], BF16) for i in range(2)]
            env["psA"] = [ps(f"psA{i}", [128, 512], F32) for i in range(2)]
            env["psX"] = [ps(f"psX{i}", [128, 512], F32) for i in range(2)]
            env["psA_i"] = [0]
            env["psX_i"] = [0]
            X = [sb(f"X{i}", [N1, G, 128], BF16) for i in range(2)]
            bias = sb("bias", [128, 1024], F32)
            hst = [sb(f"hst{i}", [128, KB, 2, G], BF16) for i in range(2)]
            hss = [sb(f"hss{i}", [128, KB, 2, G], BF16) for i in range(2)]
            sc.dma("sp", bias[:, :], bass.AP(self.hy_bias.tensor, li * 1024, [[0, 128], [1, 1024]]), writes=["bias"])
            hi_ = [0]
            plist = [(o, g) for o in range(2) for g in range(NG)]

            def genv(i):
                e2 = dict(env)
                e2["A"], e2["A2"] = Ab[i % 2], A2b[i % 2]
                e2["akey"], e2["a2key"] = ("A", i % 2), ("A2", i % 2)
                return e2

            def stage(i, part):
                o, g = plist[i]
                xi = i % 2
                if part == 1:
                    ch, p0 = divmod(g * G, 128)
                    src = bass.AP(d["kT"].tensor, ((o * 4 + ch) * 128 + p0) * 2 * L, [[128, N1], [2 * L, G], [1, 128]])
                    sc.dma("sp", X[xi][:, :, :], src, writes=[("X", xi)])

                def consume(k0, nk, bank, kb):
                    hi = hi_[0] % 2
                    hi_[0] += 1
                    cg = o * 512 + g * G
                    pr = sbap(bank, 0, 128, 0, [[2 * G, nk], [1, G]])
                    pim = sbap(bank, 0, 128, G, [[2 * G, nk], [1, G]])
                    bb = sbap(bias, 0, 128, cg, [[0, nk], [1, G]])
                    sc.add("dve", lambda e: e.tensor_tensor(out=hst[hi][:, 0:nk, 0, :], in0=pr, in1=bb, op=ALU.add),
                           reads=[kb, "bias"], writes=[Part(("hst", hi), 0)])
                    sc.add("dve", lambda e: e.tensor_copy(out=hst[hi][:, 0:nk, 1, :], in_=pim),
                           reads=[kb], writes=[Part(("hst", hi), 1)])
                    sc.add("dve", lambda e: e.tensor_tensor(out=hss[hi][:, 0:nk, 1, :], in0=pr, in1=bb, op=ALU.add),
                           reads=[kb, "bias"], writes=[Part(("hss", hi), 1)])
                    sc.add("dve", lambda e: e.tensor_copy(out=hss[hi][:, 0:nk, 0, :], in_=pim),
                           reads=[kb], writes=[Part(("hss", hi), 0)])
                    sc.dma("sp", d["Hh"][o, g, :, k0:k0 + nk, :, :], hst[hi][:, 0:nk, :, :], reads=[("hst", hi)])
                    sc.dma("sp", d["Hs"][o, g, :, k0:k0 + nk, :, :], hss[hi][:, 0:nk, :, :], reads=[("hss", hi)])
                self._fft_fwd(sc, genv(i), X[xi], ("X", xi), N1, consume, part=part)

            NPASS = len(plist)
            stage(0, 1)
            for i in range(NPASS):
                if i + 1 < NPASS:
                    stage(i + 1, 1)
                stage(i, 2)
            sc.run()

    def phase_sconv(self, li, s):
        nc, cfg = self.nc, self.cfg
        L = cfg.seq_lens[s]
        S = self.S[s]
        PW = min(L, 2048)
        with ExitStack() as es:
            sb = lambda n, shp, dt: es.enter_context(nc.sbuf_tensor(f"SC{li}{s}_{n}", shp, dt))
            cw = sb("cw", [128, 12, 3], F32)
            cb = sb("cb", [128, 12], F32)
            hyc = [sb(f"hyc{i}", [128, L + 2], BF16) for i in range(2)]
            u32 = [sb(f"u32_{i}", [128, PW], F32) for i in range(2)]
            ub = [sb(f"ub{i}", [128, L], BF16) for i in range(2)]
            sc = Sched(nc, f"SC{li}{s}")
            sc.dma("sp", cw[:, :, :], self.hy_cw[li], writes=["cw"])
            sc.dma("sp", cb[:, :], self.hy_cb[li], writes=["cb"])
            pi = 0
            for j in range(12):
                i = j % 2
                sc.dma("sp", hyc[i][:, :], S["hy"][j], writes=[("hyc", i)])
                for t0 in range(0, L, PW):
                    ui = pi % 2
                    pi += 1
                    sc.add("act", lambda e, i=i, ui=ui, j=j, t0=t0: e.activation(out=u32[ui][:, :], in_=hyc[i][:, 1 + t0:1 + t0 + PW], func=AF.Identity,
                                                                              bias=cb[:, j:j + 1], scale=cw[:, j, 1:2]),
                           reads=[("hyc", i), "cw", "cb"], writes=[("u32", ui)])
                    sc.add("dve", lambda e, i=i, ui=ui, j=j, t0=t0: e.scalar_tensor_tensor(out=u32[ui][:, :], in0=hyc[i][:, t0:t0 + PW], scalar=cw[:, j, 0:1],
                                                                                       in1=u32[ui][:, :], op0=ALU.mult, op1=ALU.add),
                           reads=[("hyc", i), "cw", ("u32", ui)], writes=[("u32", ui)])
                    sc.add("dve", lambda e, i=i, ui=ui, j=j, t0=t0: e.scalar_tensor_tensor(out=ub[i][:, t0:t0 + PW], in0=hyc[i][:, 2 + t0:2 + t0 + PW], scalar=cw[:, j, 2:3],
                                                                                       in1=u32[ui][:, :], op0=ALU.mult, op1=ALU.add),
                           reads=[("hyc", i), "cw", ("u32", ui)], writes=[Part(("ub", i), t0)])
                sc.dma("sp", S["uT"][j], ub[i][:, :], reads=[("ub", i)])
            sc.run()

    def phase_hconv(self, li, s):
        nc, cfg = self.nc, self.cfg
        L = cfg.seq_lens[s]
        S = self.S[s]
        d = self.HT[L]
        G = d["G"]
        N1, NK1, Na = d["N1"], d["NK1"], d["Na"]
        NG = 512 // G
        KB = 512 // (2 * G)
        BB = 512 // G
        with ExitStack() as es:
            sb = lambda n, shp, dt: es.enter_context(nc.sbuf_tensor(f"HC{li}{s}_{n}", shp, dt))
            ps = lambda n, shp, dt: es.enter_context(nc.psum_tensor(f"HC{li}{s}_{n}", shp, dt))
            sc = Sched(nc, f"HC{li}{s}")
            env = self._fft_consts(sc, sb, L)
            env["A"] = sb("A", [128, NK1, 2, G], BF16)
            env["A2"] = sb("A2", [128, NK1, 2, G], BF16)
            env["Bp"] = sb("Bp", [NK1, 128, 2, G], BF16)
            env["C1t"] = sb("C1t", [128, 256], BF16)
            env["C2t"] = sb("C2t", [128, 256], BF16)
            env["Trt"] = sb("Trt", [NK1, 128, Na], BF16)
            env["Tint"] = sb("Tint", [NK1, 128, Na], BF16)
            sc.dma("sp", env["C1t"][:, :], self.C1[:, :], writes=["C1t"])
            sc.dma("sp", env["C2t"][:, :], self.C2[:, :], writes=["C2t"])
            sc.dma("sp", env["Trt"][:, :, :], d["Tr"][:, :, :], writes=["Trt"])
            sc.dma("sp", env["Tint"][:, :, :], d["Tin"][:, :, :], writes=["Tint"])
            for nm in ("psA", "psX", "psB", "psY"):
                env[nm] = [ps(f"{nm}{i}", [128, 512], F32) for i in range(2)]
                env[nm + "_i"] = [0]
            Xv = [sb(f"Xv{i}", [Na, G, 128], BF16) for i in range(2)]
            Xg = [sb(f"Xg{i}", [Na, G, 128], BF16) for i in range(2)]
            Z1 = [sb(f"Z1_{i}", [Na, G, 128], BF16) for i in range(2)]
            Hh = [sb(f"Hh{i}", [128, NK1, 2, G], BF16) for i in range(2)]
            Hs = [sb(f"Hs{i}", [128, NK1, 2, G], BF16) for i in range(2)]
            Y = [sb(f"Y{i}", [128, NK1, 2, G], BF16) for i in range(2)]
            P1 = [sb(f"P1_{i}", [128, KB, 2, G], F32) for i in range(2)]
            P2 = [sb(f"P2_{i}", [128, KB, 2, G], F32) for i in range(2)]
            pcnt = [0]

            def xload(dst, key, base_chunk, g):
                ch, p0 = divmod(g * G, 128)
                src = bass.AP(S["uT"].tensor, ((base_chunk + ch) * 128 + p0) * L, [[128, Na], [L, G], [1, 128]])
                sc.dma("sp", dst[:, :, :], src, writes=[key])

            passes = []
            for gp in range(0, NG, 2):
                for o in range(2):
                    for g in (gp, gp + 1):
                        if g < NG:
                            passes.append((g, o))

            def fwd(pi, part):
                g, o = passes[pi]
                vi = g % 2
                hi = pi % 2
                if part == 1:
                    if o == 0:
                        xload(Xv[vi], ("Xv", vi), 0, g)
                    xload(Xg[hi], ("Xg", hi), 4 * (o + 1), g)
                    sc.dma("sp", Hh[hi][:, :, :, :], d["Hh"][o, g], writes=[("Hh", hi)])
                    sc.dma("sp", Hs[hi][:, :, :, :], d["Hs"][o, g], writes=[("Hs", hi)])
                Xin, xkey = (Xv[vi], ("Xv", vi)) if o == 0 else (Z1[vi], ("Z1", vi))
                Yb = Y[hi]

                def consume(k0, nk, bank, kb):
                    pi_ = pcnt[0] % 2
                    pcnt[0] += 1
                    pb = sbap(bank, 0, 128, 0, [[1, nk * 2 * G]])
                    sc.add("dve", lambda e: e.tensor_tensor(
                        out=sbap(P1[pi_], 0, 128, 0, [[1, nk * 2 * G]]), in0=pb, in1=sbap(Hh[hi], 0, 128, k0 * 2 * G, [[1, nk * 2 * G]]), op=ALU.mult),
                        reads=[kb, ("Hh", hi)], writes=[("P1", pi_)])
                    sc.add("dve", lambda e: e.tensor_tensor(
                        out=sbap(P2[pi_], 0, 128, 0, [[1, nk * 2 * G]]), in0=pb, in1=sbap(Hs[hi], 0, 128, k0 * 2 * G, [[1, nk * 2 * G]]), op=ALU.mult),
                        reads=[kb, ("Hs", hi)], writes=[("P2", pi_)])
                    sc.add("pool", lambda e: e.tensor_tensor(
                        out=Yb[:, k0:k0 + nk, 0, :], in0=P1[pi_][:, 0:nk, 0, :], in1=P1[pi_][:, 0:nk, 1, :], op=ALU.subtract),
                        reads=[("P1", pi_)], writes=[Part(("Y", hi), (k0, 0))])
                    sc.add("pool", lambda e: e.tensor_tensor(
                        out=Yb[:, k0:k0 + nk, 1, :], in0=P2[pi_][:, 0:nk, 0, :], in1=P2[pi_][:, 0:nk, 1, :], op=ALU.add),
                        reads=[("P2", pi_)], writes=[Part(("Y", hi), (k0, 1))])
                self._fft_fwd(sc, env, Xin, xkey, Na, consume, part=part)

            def inv(pi, part):
                g, o = passes[pi]
                vi = g % 2
                hi = pi % 2
                if o == 0:
                    dst, dkey = Z1[vi], ("Z1", vi)
                else:
                    dst, dkey = Xv[vi], ("Xv", vi)

                def out_cb(b0, nb, bank, kb):
                    sc.add("dve", lambda e: e.tensor_tensor(
                        out=sbap(dst, 0, Na, b0, [[128, G], [1, nb]]), in0=sbap(bank, 0, Na, 0, [[1, G], [G, nb]]),
                        in1=sbap(Xg[hi], 0, Na, b0, [[128, G], [1, nb]]), op=ALU.mult),
                        reads=[kb, ("Xg", hi)], writes=[Part(dkey, b0)])
                self._fft_inv(sc, env, Y[hi], ("Y", hi), out_cb, part=part)
                if o == 1 and part == 2:
                    ch, p0 = divmod(g * G, 128)
                    dstap = bass.AP(S["zT"].tensor, (ch * 128 + p0) * L, [[128, Na], [L, G], [1, 128]])
                    sc.dma("act", dstap, Xv[vi][:, :, :], reads=[("Xv", vi)])

            NP = len(passes)
            fwd(0, 1)
            fwd(0, 2)
            for pi in range(NP):
                nxt = pi + 1 < NP
                if nxt:
                    fwd(pi + 1, 1)
                inv(pi, 1)
                if nxt:
                    fwd(pi + 1, 2)
                inv(pi, 2)
            sc.run()

    def build(self):
        cfg = self.cfg
        self.declare()
        only = getattr(cfg, "only", None)
        self.declare_hy_scratch()
        with ExitStack() as es:
            GLOB[0] = Sched.make_glob(self.nc, es)
            for li in range(cfg.depth):
                if "hy" in cfg.mixers:
                    for L in cfg.ulens:
                        if only is None or "D" in only or "D0" in only:
                            self.phase_filt(li, L)
                        if only is None or "D" in only or "D1" in only:
                            self.phase_fspec(li, L)
                allseq = list(range(len(cfg.seq_lens)))
                if only is None or "A" in only:
                    self.phase_inproj(li, allseq)
                for s in allseq:
                    if "na" in cfg.mixers and (only is None or "B" in only):
                        self.phase_na(li, s)
                    if "swa" in cfg.mixers and (only is None or "C" in only):
                        self.phase_swa(li, s)
                    if "hy" in cfg.mixers and (only is None or "D" in only or "D2" in only):
                        self.phase_sconv(li, s)
                    if "hy" in cfg.mixers and (only is None or "D" in only or "D3" in only):
                        self.phase_hconv(li, s)
                if only is None or "E" in only:
                    self.phase_merge(li, allseq)
                if only is None or "F" in only:
                    self.phase_mlp(li, allseq)
        return self.nc


def make_in_maps(cfg, inputs, n_cores, seq_arrays):
    wl = host_layout_weights(inputs, cfg.depth)
    consts = host_constants(cfg)
    maps = []
    for c in range(n_cores):
        m = dict(wl)
        m.update(consts)
        for s, x in enumerate(seq_arrays[c]):
            m[f"x{s}"] = np.ascontiguousarray(x, dtype=np.float32)
        maps.append(m)
    return maps


def kernel(**inputs):
    cfg = Cfg()
    b = Builder(cfg)
    nc = b.build()
    xp = np.asarray(inputs["x_prompt"], np.float32)
    xsm = np.asarray(inputs["x_sample"], np.float32)
    n = 8
    seqs = [[xp[c], xsm[2 * c], xsm[2 * c + 1]] for c in range(n)]
    in_maps = make_in_maps(cfg, inputs, n, seqs)
    res = run_bass_kernel_spmd(nc, in_maps, core_ids=list(range(n)))
    yp = np.stack([np.asarray(res.results[c]["y0"], np.float32) for c in range(n)])
    ys = np.stack([np.asarray(res.results[c][f"y{1 + j}"], np.float32) for c in range(n) for j in range(2)])
    return (yp, ys)
```

```python
import math
from contextlib import ExitStack

import numpy as np
import concourse.bass as bass
import concourse.mybir as mybir
from concourse.bass_utils import run_bass_kernel_spmd

F32 = mybir.dt.float32
BF16 = mybir.dt.bfloat16
AF = mybir.ActivationFunctionType
ALU = mybir.AluOpType

D_MODEL = 1024
DEPTH = 2
HEAD_DIM = 64
GRID_W = 64
NA_HEADS = 8
NA_WIN_ROWS = 8
NA_WIN_COLS = 16
NA_WIDTH = 512
HY_WIDTH = 512
HY_FILTER_HIDDEN = 64
HY_EMB_DIM = 33
HY_POS_BANDS = 16
SWA_HEADS = 8
SWA_KV_HEADS = 2
SWA_WIDTH = 512
SWA_KV_WIDTH = 128
SWA_WINDOW = 128
D_FF = 4096
OFF_HY = 3 * NA_WIDTH
OFF_SWA = OFF_HY + 3 * HY_WIDTH
OFF_GATE = OFF_SWA + SWA_WIDTH + 2 * SWA_KV_WIDTH
D_IN = OFF_GATE + 3 * D_MODEL
ALPHA = (2 * DEPTH) ** 0.25
LN_EPS = 1e-5
NEG = -30000.0

SAME_ENGINE_SYNC = True
GLOB = [None]


class Part:
    __slots__ = ("key", "sub")

    def __init__(self, key, sub):
        self.key = key
        self.sub = sub


class _Op:
    __slots__ = ("eng", "emit", "deps", "is_dma", "signal", "sem", "val", "idx", "prev_val", "raw")

    def __init__(self, eng, emit, is_dma, idx):
        self.eng = eng
        self.emit = emit
        self.deps = []
        self.is_dma = is_dma
        self.signal = is_dma
        self.sem = None
        self.val = 0
        self.prev_val = 0
        self.idx = idx
        self.raw = set()


class Sched:
    ENGS = ("pe", "act", "dve", "pool", "sp")
    NDMA = {"sp": 16, "act": 12, "pool": 8}

    def __init__(self, nc, name, glob=None):
        self.nc = nc
        self.name = name
        self.ops = []
        self.state = {}
        self.glob = glob if glob is not None else GLOB[0]

    @classmethod
    def make_glob(cls, nc, es):
        csem = {e: es.enter_context(nc.semaphore(f"c_{e}")) for e in ("pe", "act", "dve", "pool")}
        dsem = {q: [es.enter_context(nc.semaphore(f"d_{q}{i}")) for i in range(n)] for q, n in cls.NDMA.items()}
        return {"csem": csem, "dsem": dsem, "cnt": {e: 0 for e in csem},
                "duse": {q: [0] * n for q, n in cls.NDMA.items()}, "drr": {q: 0 for q in cls.NDMA}}

    @staticmethod
    def _conf(a, b):
        return a is None or b is None or a == b

    def _touch(self, op, item, write, rd_excl=False):
        if isinstance(item, Part):
            key, sub = item.key, item.sub
        else:
            key, sub = item, None
        st = self.state.get(key)
        if st is None:
            st = self.state[key] = {"w": [], "r": []}
        for (s, o) in st["w"]:
            if self._conf(s, sub):
                op.deps.append(o)
                if not (write and not rd_excl):
                    op.raw.add(id(o))
        if write:
            for (s, o) in st["r"]:
                if self._conf(s, sub):
                    op.deps.append(o)
            if sub is None:
                st["w"] = [(None, op)]
                st["r"] = []
            else:
                st["w"] = [(s, o) for (s, o) in st["w"] if s != sub] + [(sub, op)]
                st["r"] = [(s, o) for (s, o) in st["r"] if s != sub]
        else:
            if not op.is_dma:
                st["r"] = [(s, o) for (s, o) in st["r"]
                           if not (s == sub and o.eng == op.eng and not o.is_dma)]
            st["r"].append((sub, op))

    PSUM_KEYS = ("acc", "pT", "S", "O", "pm", "p3", "psA", "psX", "psB", "psY")

    def _is_psum(self, item):
        key = item.key if isinstance(item, Part) else item
        name = key[0] if isinstance(key, tuple) else key
        return name in self.PSUM_KEYS

    def add(self, eng, emit, reads=(), writes=(), dma=False):
        op = _Op(eng, emit, dma, len(self.ops))
        for r in reads:
            if self._is_psum(r):
                k = r.key if isinstance(r, Part) else r
                self._touch(op, k, True, rd_excl=True)
            else:
                self._touch(op, r, False)
        for w in writes:
            self._touch(op, w, True)
        self.ops.append(op)
        return op

    def dma(self, queue, out, in_, reads=(), writes=(), **kw):
        return self.add(queue, lambda e, o=out, i=in_: e.dma_start(out=o, in_=i, **kw), reads, writes, dma=True)

    def run(self):
        nc = self.nc
        per = {e: [] for e in self.ENGS}
        for op in self.ops:
            seen = set()
            deps = []
            for d in op.deps:
                if d is op or id(d) in seen:
                    continue
                seen.add(id(d))
                deps.append(d)
            op.deps = deps
            per[op.eng].append(op)
            for d in deps:
                if d.is_dma:
                    continue
                if d.eng == op.eng and not op.is_dma and (op.eng == "pe" or not SAME_ENGINE_SYNC or id(d) not in op.raw):
                    continue
                d.signal = True
        for e in self.ENGS:
            for op in reversed(per[e]):
                if not op.is_dma:
                    op.signal = True
                    break
        with ExitStack() as es:
            G = self.glob
            csem, dsem, cnt, duse, drr = G["csem"], G["dsem"], G["cnt"], G["duse"], G["drr"]
            for e in self.ENGS:
                for op in per[e]:
                    if op.is_dma:
                        k = drr[e]
                        drr[e] = (k + 1) % self.NDMA[e]
                        op.prev_val = 16 * duse[e][k]
                        duse[e][k] += 1
                        op.sem = dsem[e][k]
                        op.val = 16 * duse[e][k]
                    elif op.signal:
                        cnt[e] += 1
                        op.sem = csem[e]
                        op.val = cnt[e]
            finals = []
            for e in csem:
                if cnt[e]:
                    finals.append((csem[e], cnt[e]))
            for q in self.NDMA:
                for k, n in enumerate(duse[q]):
                    if n:
                        finals.append((dsem[q][k], 16 * n))
            block = es.enter_context(nc.Block())

            def body(ename):
                def f(eh):
                    waited = {}

                    def wait(sem, val):
                        if val <= 0:
                            return
                        if waited.get(id(sem), 0) >= val:
                            return
                        waited[id(sem)] = val
                        eh.wait_ge(sem, val)

                    for op in per[ename]:
                        for d in op.deps:
                            if (not d.is_dma) and d.eng == ename and not op.is_dma and \
                                    (ename == "pe" or not SAME_ENGINE_SYNC or id(d) not in op.raw):
                                continue
                            wait(d.sem, d.val)
                        if op.is_dma:
                            wait(op.sem, op.prev_val)
                        ins = op.emit(eh)
                        if op.signal:
                            ins.then_inc(op.sem, 16 if op.is_dma else 1)
                    for (sem, val) in finals:
                        wait(sem, val)
                return f

            block.tensor(body("pe"))
            block.scalar(body("act"))
            block.vector(body("dve"))
            block.gpsimd(body("pool"))
            block.sync(body("sp"))


def sbap(t, part0, nparts, off, dims):
    row = 1
    for s in list(t.shape)[1:]:
        row *= int(s)
    return bass.AP(t, part0 * row + off, [[row, nparts]] + [list(d) for d in dims])


class Cfg:
    def __init__(self, seq_lens=(8192, 2048, 2048), depth=DEPTH, mixers=("na", "hy", "swa")):
        self.seq_lens = tuple(seq_lens)
        self.depth = depth
        self.mixers = tuple(mixers)
        self.ulens = sorted(set(self.seq_lens), reverse=True)


FM_CHUNKS = []
for j in range(4):
    FM_CHUNKS.append(("qn", j, [(j * 128, 128)], "id"))
for j in range(4):
    FM_CHUNKS.append(("kn", j, [(NA_WIDTH + j * 128, 128)], "id"))
for j in range(12):
    FM_CHUNKS.append(("hy", j, [(OFF_HY + j * 128, 128)], "id"))
for j in range(4):
    FM_CHUNKS.append(("qs", j, [(OFF_SWA + j * 128, 128)], "id"))
for g in range(2):
    c0 = OFF_SWA + SWA_WIDTH + g * 64
    FM_CHUNKS.append(("ks", g, [(c0, 64), (c0, 64)], "id"))
for j in range(24):
    FM_CHUNKS.append(("g", j, [(OFF_GATE + j * 128, 128)], "sig"))
NFM = len(FM_CHUNKS)
TM_GROUPS = [
    ("vn", [(2 * NA_WIDTH, 512)]),
    ("vs", [(OFF_SWA + SWA_WIDTH + SWA_KV_WIDTH, 64), (OFF_SWA + SWA_WIDTH + SWA_KV_WIDTH, 64),
            (OFF_SWA + SWA_WIDTH + SWA_KV_WIDTH + 64, 64), (OFF_SWA + SWA_WIDTH + SWA_KV_WIDTH + 64, 64)]),
]
TM_OFF = [NFM * 128, NFM * 128 + 512]
TM_N = [512, 256]
WA_COLS = NFM * 128 + 512 + 256


def _col_index():
    idx = []
    for (_, _, rngs, _) in FM_CHUNKS:
        for (c0, n) in rngs:
            idx.extend(range(c0, c0 + n))
    for (_, rngs) in TM_GROUPS:
        for (c0, n) in rngs:
            idx.extend(range(c0, c0 + n))
    return np.asarray(idx, dtype=np.int64)


COL_INDEX = _col_index()


def kmajor(w):
    K, N = w.shape
    return np.ascontiguousarray(w.reshape(K // 128, 128, N).transpose(1, 0, 2))


def host_layout_weights(inp, depth):
    out = {}
    w_in = np.asarray(inp["w_in"], np.float32)
    b_in = np.asarray(inp["b_in"], np.float32)
    out["wA"] = np.stack([kmajor(w_in[l][:, COL_INDEX]) for l in range(depth)])
    bA = np.stack([b_in[l][COL_INDEX] for l in range(depth)])
    out["bA_fm"] = np.ascontiguousarray(bA[:, :NFM * 128].reshape(depth, NFM, 128).transpose(0, 2, 1))
    out["bA_tm"] = np.ascontiguousarray(bA[:, NFM * 128:].reshape(depth, 1, 768))
    for nm in ("w_branch_a", "w_branch_b", "w_branch_c", "w_out", "w_up", "w_down"):
        w = np.asarray(inp[nm], np.float32)
        out[nm] = np.stack([kmajor(w[l]) for l in range(depth)])
    out["b_up"] = np.ascontiguousarray(np.asarray(inp["b_up"], np.float32)[:depth].reshape(depth, 32, 128).transpose(0, 2, 1))
    f = lambda n: np.asarray(inp[n], np.float32)[:depth]
    out["hy_w1"] = np.ascontiguousarray(f("hy_filt_w1"))
    out["hy_w2"] = np.ascontiguousarray(f("hy_filt_w2"))
    out["hy_w3"] = np.ascontiguousarray(f("hy_filt_w3"))
    out["hy_b1"] = np.ascontiguousarray(f("hy_filt_b1").reshape(depth, 64, 1))
    out["hy_b2"] = np.ascontiguousarray(f("hy_filt_b2").reshape(depth, 64, 1))
    out["hy_fr"] = np.ascontiguousarray(f("hy_filt_freq").reshape(depth, 64, 1))
    out["hy_bias"] = np.ascontiguousarray(f("hy_bias").reshape(depth, 1, 1024))
    out["hy_cw"] = np.ascontiguousarray(f("hy_conv_w").reshape(depth, 3, 12, 128).transpose(0, 3, 2, 1))
    out["hy_cb"] = np.ascontiguousarray(f("hy_conv_b").reshape(depth, 12, 128).transpose(0, 2, 1))
    out["na_tab"] = host_na_tables(inp["na_rpb"], depth)
    out["swa_sink"] = np.ascontiguousarray(np.asarray(inp["swa_sink"], np.float32)[:depth].reshape(depth, 1, 8))
    for nm in ("b_down", "ln1_g", "ln1_b", "ln2_g", "ln2_b"):
        out[nm] = np.ascontiguousarray(np.asarray(inp[nm], np.float32)[:depth].reshape(depth, 1, 1024))
    return out


def _na_inwin():
    c = np.arange(64)[:, None]
    w = np.arange(64)[None, :]
    cs = np.clip(w - NA_WIN_COLS // 2, 0, GRID_W - NA_WIN_COLS)
    return (c >= cs) & (c < cs + NA_WIN_COLS)


def host_na_tables(rpb, depth):
    rpb = np.asarray(rpb, np.float32)
    inw = _na_inwin()
    c = np.arange(64)[:, None]
    w = np.arange(64)[None, :]
    cidx = np.clip(c - w + (NA_WIN_COLS - 1), 0, 2 * NA_WIN_COLS - 2)
    out = np.zeros((depth, 2, 128, NA_HEADS, 8, 64), np.float32)
    for l in range(depth):
        for t in range(2):
            for par in range(2):
                for m in range(8):
                    rho = 2 * m + par + t
                    if rho >= 2 * NA_WIN_ROWS - 1:
                        continue
                    g = rpb[l][:, rho, :][:, cidx]
                    g = np.where(inw[None], g, np.float32(0.0))
                    out[l, t, par * 64:(par + 1) * 64, :, m, :] = g.transpose(1, 0, 2)
    return out


def host_constants(cfg):
    c = {}
    import ml_dtypes
    bf = ml_dtypes.bfloat16
    c["ident"] = np.eye(128, dtype=np.float32).astype(bf)
    inw = _na_inwin()
    mk = np.where(inw, 0.0, NEG).astype(np.float32)
    c["na_mask"] = np.concatenate([mk, mk], axis=0)
    slopes = 2.0 ** (-8.0 * (np.arange(SWA_HEADS, dtype=np.float64) + 1.0) / SWA_HEADS)
    k = np.arange(128)[:, None, None, None]
    h = np.arange(8)[None, :, None, None]
    ch = np.arange(3)[None, None, :, None] - 1
    q = np.arange(128)[None, None, None, :]
    rel = np.abs(q - (k + 128 * ch))
    val = np.where(rel <= SWA_WINDOW, -slopes[h] * rel * (HEAD_DIM ** 0.5), NEG)
    hi = val.astype(np.float32).astype(bf)
    lo = (val - hi.astype(np.float64)).astype(np.float32).astype(bf)
    c["al_hi"] = hi
    c["al_lo"] = lo
    min_decay = math.log(1e-2) / 1.5
    max_decay = math.log(1e-2) / 0.3
    deltas = np.abs(np.linspace(min_decay, max_decay, HY_WIDTH, dtype=np.float32)).astype(np.float32)
    c["absdelta"] = np.ascontiguousarray(deltas.reshape(4, 128).T)
    for L in cfg.ulens:
        N1 = 2 * L // 128
        NK1 = N1 // 2 + 1
        Na = N1 // 2
        N = 2 * L
        pos = np.concatenate([np.arange(L), [0], np.arange(L - 1, 0, -1)]).astype(np.int64)
        t = np.linspace(0.0, 1.0, L, dtype=np.float32)
        wv = (2.0 * np.float32(math.pi) * np.arange(L, dtype=np.float32) / np.float32(L)).astype(np.float32)
        bands = np.linspace(1e-4, HY_POS_BANDS - 1, HY_POS_BANDS, dtype=np.float32)
        ang = (wv[:, None] * bands[None, :]).astype(np.float32)
        z = np.concatenate([t[:, None], np.cos(ang), -np.sin(ang)], axis=-1).astype(np.float32)
        c[f"zz{L}"] = np.ascontiguousarray(z[pos].T)
        tt = -t[pos]
        tt[L] = -1.0e4
        c[f"tneg{L}"] = np.ascontiguousarray(tt.reshape(1, 2 * L).astype(np.float32))
        a = np.arange(N1, dtype=np.float64)[:, None]
        k1 = np.arange(NK1, dtype=np.float64)[None, :]
        ang1 = 2 * np.pi * a * k1 / N1
        F1 = np.stack([np.cos(ang1), -np.sin(ang1)], axis=-1).reshape(N1, 2 * NK1)
        c[f"F1_{L}"] = F1.astype(np.float32).astype(bf)
        b = np.arange(128, dtype=np.float64)[:, None, None]
        k1 = np.arange(NK1, dtype=np.float64)[None, :, None]
        k2 = np.arange(128, dtype=np.float64)[None, None, :]
        angm = -2 * np.pi * b * (k1 + N1 * k2) / N
        c[f"Mr_{L}"] = np.cos(angm).astype(np.float32).astype(bf)
        c[f"Mi_{L}"] = np.sin(angm).astype(np.float32).astype(bf)
        k1 = np.arange(NK1, dtype=np.float64)[:, None, None]
        bb = np.arange(128, dtype=np.float64)[None, :, None]
        aa = np.arange(Na, dtype=np.float64)[None, None, :]
        ck = np.where((k1 == 0) | (k1 == N1 // 2), 1.0, 2.0) / N1
        angt = 2 * np.pi * (aa * k1 / N1 + bb * k1 / N)
        c[f"Tr_{L}"] = (ck * np.cos(angt)).astype(np.float32).astype(bf)
        c[f"Tin_{L}"] = (-ck * np.sin(angt)).astype(np.float32).astype(bf)
    k2 = np.arange(128, dtype=np.float64)[:, None]
    b = np.arange(128, dtype=np.float64)[None, :]
    angc = 2 * np.pi * b * k2 / 128
    Cr = np.cos(angc) / 128
    Ci = np.sin(angc) / 128
    c["C1"] = np.concatenate([Cr, Ci], axis=1).astype(np.float32).astype(bf)
    c["C2"] = np.concatenate([-Ci, Cr], axis=1).astype(np.float32).astype(bf)
    return c


class Builder:
    def __init__(self, cfg):
        self.cfg = cfg
        self.nc = bass.Bass("TRN2", target_bir_lowering=False)
        self.ext_in = {}
        self.dram = {}

    def inp(self, name, shape, dt):
        t = self.nc.dram_tensor(name, list(shape), dt, kind="ExternalInput").ap()
        self.ext_in[name] = t
        return t

    def outp(self, name, shape, dt):
        return self.nc.dram_tensor(name, list(shape), dt, kind="ExternalOutput").ap()

    def scratch(self, name, shape, dt):
        t = self.nc.dram_tensor(name, list(shape), dt).ap()
        self.dram[name] = t
        return t

    def bvec(self, name, li, n=1024):
        return bass.AP(self.vecs[name].tensor, li * n, [[0, 128], [1, n]])

    def declare(self):
        cfg = self.cfg
        D = cfg.depth
        self.x_in = [self.inp(f"x{s}", [L, D_MODEL], F32) for s, L in enumerate(cfg.seq_lens)]
        self.y_out = [self.outp(f"y{s}", [L, D_MODEL], F32) for s, L in enumerate(cfg.seq_lens)]
        self.wA = self.inp("wA", [D, 128, 8, WA_COLS], F32)
        self.bA_fm = self.inp("bA_fm", [D, 128, NFM], F32)
        self.bA_tm = self.inp("bA_tm", [D, 1, 768], F32)
        self.w_br = [self.inp(n, [D, 128, 4, 1024], F32) for n in ("w_branch_a", "w_branch_b", "w_branch_c")]
        self.w_out = self.inp("w_out", [D, 128, 8, 1024], F32)
        self.w_up = self.inp("w_up", [D, 128, 8, 4096], F32)
        self.w_down = self.inp("w_down", [D, 128, 32, 1024], F32)
        self.b_up = self.inp("b_up", [D, 128, 32], F32)
        self.vecs = {n: self.inp(n, [D, 1, 1024], F32) for n in ("b_down", "ln1_g", "ln1_b", "ln2_g", "ln2_b")}
        self.ident = self.inp("ident", [128, 128], BF16)
        self.na_tab = self.inp("na_tab", [D, 2, 128, 8, 8, 64], F32)
        self.na_mask = self.inp("na_mask", [128, 64], F32)
        self.swa_sink = self.inp("swa_sink", [D, 1, 8], F32)
        self.al_hi = self.inp("al_hi", [128, 8, 3, 128], BF16)
        self.al_lo = self.inp("al_lo", [128, 8, 3, 128], BF16)
        self.hy_w1 = self.inp("hy_w1", [D, 33, 64], F32)
        self.hy_w2 = self.inp("hy_w2", [D, 64, 64], F32)
        self.hy_w3 = self.inp("hy_w3", [D, 64, 2048], F32)
        self.hy_b1 = self.inp("hy_b1", [D, 64, 1], F32)
        self.hy_b2 = self.inp("hy_b2", [D, 64, 1], F32)
        self.hy_fr = self.inp("hy_fr", [D, 64, 1], F32)
        self.hy_bias = self.inp("hy_bias", [D, 1, 1024], F32)
        self.hy_cw = self.inp("hy_cw", [D, 128, 12, 3], F32)
        self.hy_cb = self.inp("hy_cb", [D, 128, 12], F32)
        self.absdelta = self.inp("absdelta", [128, 4], F32)
        self.C1 = self.inp("C1", [128, 256], BF16)
        self.C2 = self.inp("C2", [128, 256], BF16)
        self.HT = {}
        for L in cfg.ulens:
            N1 = 2 * L // 128
            NK1 = N1 // 2 + 1
            Na = N1 // 2
            d = {"N1": N1, "NK1": NK1, "Na": Na}
            d["zz"] = self.inp(f"zz{L}", [33, 2 * L], F32)
            d["tneg"] = self.inp(f"tneg{L}", [1, 2 * L], F32)
            d["F1"] = self.inp(f"F1_{L}", [N1, 2 * NK1], BF16)
            d["Mr"] = self.inp(f"Mr_{L}", [128, NK1, 128], BF16)
            d["Mi"] = self.inp(f"Mi_{L}", [128, NK1, 128], BF16)
            d["Tr"] = self.inp(f"Tr_{L}", [NK1, 128, Na], BF16)
            d["Tin"] = self.inp(f"Tin_{L}", [NK1, 128, Na], BF16)
            d["kT"] = self.scratch(f"kT{L}", [8, 128, 2 * L], BF16)
            d["H"] = self.scratch(f"H{L}", [128, 2, NK1, 2, 512], BF16)
            self.HT[L] = d
        self.S = []
        for s, L in enumerate(cfg.seq_lens):
            d = {}
            d["xmid"] = self.scratch(f"xmid{s}", [L, D_MODEL], F32)
            d["x1"] = self.scratch(f"x1_{s}", [L, D_MODEL], F32)
            d["qn"] = self.scratch(f"qn{s}", [4, 128, L], BF16)
            d["kn"] = self.scratch(f"kn{s}", [4, 128, L], BF16)
            d["vn"] = self.scratch(f"vn{s}", [L, 512], BF16)
            d["hy"] = self.scratch(f"hy{s}", [12, 128, L + 2], BF16)
            d["qs"] = self.scratch(f"qs{s}", [4, 128, L], BF16)
            d["ks"] = self.scratch(f"ks{s}", [2, 128, L], BF16)
            d["vs"] = self.scratch(f"vs{s}", [L, 256], BF16)
            d["g"] = self.scratch(f"g{s}", [24, 128, L], BF16)
            d["aT"] = self.scratch(f"aT{s}", [4, 128, L], BF16)
            d["zT"] = self.scratch(f"zT{s}", [4, 128, L], BF16)
            d["cT"] = self.scratch(f"cT{s}", [4, 128, L], BF16)
            d["uT"] = self.scratch(f"uT{s}", [12, 128, L], BF16)
            self.S.append(d)

    def phase_inproj(self, li, seqs):
        s = 'x'.join(str(q) for q in seqs)
        nc, cfg = self.nc, self.cfg
        bmap = [(q, t) for q in seqs for t in range(cfg.seq_lens[q] // 512)]
        NB = len(bmap)
        with ExitStack() as es:
            sb = lambda n, shp, dt: es.enter_context(nc.sbuf_tensor(f"A{li}{s}_{n}", shp, dt))
            ps = lambda n, shp, dt: es.enter_context(nc.psum_tensor(f"A{li}{s}_{n}", shp, dt))
            w = sb("w", [128, 8, WA_COLS], BF16)
            bfm = sb("bfm", [128, NFM], F32)
            btm32 = sb("btm32", [1, 768], F32)
            btm = sb("btm", [1, 768], BF16)
            ones = sb("ones", [1, 128], BF16)
            ident = sb("ident", [128, 128], BF16)
            zcol = sb("zcol", [128, 12, 1], BF16)
            xs = [sb(f"xs{i}", [128, 1024], F32) for i in range(4)]
            xb = [sb(f"xb{i}", [128, 1024], BF16) for i in range(4)]
            xT = [sb(f"xT{i}", [128, 8, 512], BF16) for i in range(2)]
            NST = 6
            st = [sb(f"st{i}", [128, 512], BF16) for i in range(NST)]
            pT = [ps(f"pT{i}", [128, 8, 128], BF16) for i in range(4)]
            NACC = 4
            acc = [ps(f"acc{i}", [128, 512], F32) for i in range(NACC)]
            sc = Sched(nc, f"A{li}{s}")
            for k in range(8):
                sc.dma("pool", w[:, k, :], self.wA[li, :, k, :], writes=[Part("w", k)])
            sc.dma("sp", bfm[:, :], self.bA_fm[li], writes=["bfm"])
            sc.dma("sp", btm32[:, :], self.bA_tm[li], writes=["btm32"])
            sc.dma("sp", ident[:, :], self.ident[:, :], writes=["ident"])
            sc.add("dve", lambda e: e.tensor_copy(out=btm[:, :], in_=btm32[:, :]), reads=["btm32"], writes=["btm"])
            sc.add("dve", lambda e: e.memset(ones[:, :], 1.0), writes=["ones"])
            sc.add("dve", lambda e: e.memset(zcol[:, :, :], 0.0), writes=["zcol"])
            for q in seqs:
                hy = self.S[q]["hy"]
                Lq = cfg.seq_lens[q]
                sc.dma("sp", hy[:, :, 0:1].rearrange("c p o -> p c o"), zcol[:, :, :], reads=["zcol"], allow_slow_non_contiguous=True)
                sc.dma("sp", hy[:, :, Lq + 1:Lq + 2].rearrange("c p o -> p c o"), zcol[:, :, :], reads=["zcol"], allow_slow_non_contiguous=True)
            sti = 0
            acci = 0
            def prep_load(tb):
                sq, ltb = bmap[tb]
                S = self.S[sq]
                x_src = self.x_in[sq] if li == 0 else S["xmid"]
                for t in range(4):
                    tok0 = ltb * 512 + t * 128
                    sc.dma("sp", xs[t][:, :], x_src[tok0:tok0 + 128, :], writes=[("xs", t)])
                    sc.add("pool", lambda e, o=xb[t], i=xs[t]: e.tensor_copy(out=o[:, :], in_=i[:, :]),
                           reads=[("xs", t)], writes=[("xb", t)])

            def prep_tr(tb):
                xTb = xT[tb % 2]
                kx = ("xT", tb % 2)
                for t in range(4):
                    def tr(e, o=pT[t], i=xb[t]):
                        last = None
                        for k in range(8):
                            last = e.transpose(out=o[:, k, :], in_=i[:, k * 128:(k + 1) * 128], identity=ident[:, :])
                        return last
                    sc.add("pe", tr, reads=[("xb", t), "ident"], writes=[("pT", t)])
                    sc.add("dve", lambda e, o=xTb, i=pT[t], t=t: e.tensor_copy(out=o[:, :, t * 128:(t + 1) * 128], in_=i[:, :, :]),
                           reads=[("pT", t)], writes=[Part(kx, t)])

            prep_load(0)
            prep_tr(0)
            for tb in range(NB):
                sq, ltb = bmap[tb]
                S = self.S[sq]
                xTb = xT[tb % 2]
                kx = ("xT", tb % 2)
                for ci, (nm, j, _, fn) in enumerate(FM_CHUNKS):
                    if ci == 0 and tb + 1 < NB:
                        prep_load(tb + 1)
                    if ci == NFM // 2 and tb + 1 < NB:
                        prep_tr(tb + 1)
                    a = acc[acci % NACC]
                    ka = ("acc", acci % NACC)
                    acci += 1

                    def mm(e, a=a, ci=ci, xTb=xTb):
                        last = None
                        for k in range(8):
                            last = e.matmul(a[:, :], lhsT=w[:, k, ci * 128:(ci + 1) * 128], rhs=xTb[:, k, :],
                                            start=(k == 0), stop=(k == 7))
                        return last
                    sc.add("pe", mm, reads=["w", kx], writes=[ka])
                    so = st[sti % NST]
                    ks = ("st", sti % NST)
                    sti += 1
                    func = AF.Sigmoid if fn == "sig" else AF.Identity
                    sc.add("act", lambda e, so=so, a=a, ci=ci, func=func: e.activation(out=so[:, :], in_=a[:, :], func=func,
                                                                                      bias=bfm[:, ci:ci + 1], scale=1.0),
                           reads=[ka, "bfm"], writes=[ks])
                    dst = S[nm]
                    off = 1 if nm == "hy" else 0
                    sc.dma("act", dst[j, :, off + ltb * 512: off + ltb * 512 + 512], so[:, :], reads=[ks])
                for t in range(4):
                    tok0 = ltb * 512 + t * 128
                    for gi, (nm, _) in enumerate(TM_GROUPS):
                        N = TM_N[gi]
                        c0 = TM_OFF[gi]
                        a = acc[acci % NACC]
                        ka = ("acc", acci % NACC)
                        acci += 1

                        def mm2(e, a=a, c0=c0, N=N, t=t, xTb=xTb, gi=gi):
                            for k in range(8):
                                e.matmul(a[:, 0:N], lhsT=xTb[:, k, t * 128:(t + 1) * 128], rhs=w[:, k, c0:c0 + N],
                                         start=(k == 0), stop=False)
                            b0 = 0 if gi == 0 else 512
                            return e.matmul(a[:, 0:N], lhsT=ones[:, :], rhs=btm[:, b0:b0 + N], start=False, stop=True)
                        sc.add("pe", mm2, reads=["w", kx, "ones", "btm"], writes=[ka])
                        so = st[sti % NST]
                        ks = ("st", sti % NST)
                        sti += 1
                        sc.add("dve", lambda e, so=so, a=a, N=N: e.tensor_copy(out=so[:, 0:N], in_=a[:, 0:N]),
                               reads=[ka], writes=[ks])
                        sc.dma("act", S[nm][tok0:tok0 + 128, :], so[:, 0:N], reads=[ks])
            sc.run()

    def _ln(self, sc, e_stats, src, srckey, dst, dstkey, gam, bet, tmp, uid):
        stats, mv, rstd = tmp
        k_st, k_mv, k_rs = (("lnst", uid), ("lnmv", uid), ("lnrs", uid))
        for h in range(2):
            sc.add("dve", lambda e, h=h: e.bn_stats(out=stats[:, h * 6:(h + 1) * 6], in_=src[:, h * 512:(h + 1) * 512]),
                   reads=[srckey], writes=[Part(k_st, h)])
        sc.add("dve", lambda e: e.bn_aggr(out=mv[:, :], in_=stats[:, :]), reads=[k_st], writes=[k_mv])
        sc.add("act", lambda e: e.activation(out=rstd[:, :], in_=mv[:, 1:2], func=AF.Sqrt, bias=self.epsc[:, 0:1], scale=1.0),
               reads=[k_mv, "epsc"], writes=[k_rs])
        sc.add("dve", lambda e: e.reciprocal(out=rstd[:, :], in_=rstd[:, :]), reads=[k_rs], writes=[k_rs])
        sc.add("dve", lambda e: e.scalar_tensor_tensor(out=src[:, :], in0=src[:, :], scalar=mv[:, 0:1], in1=gam[:, :],
                                                       op0=ALU.subtract, op1=ALU.mult), reads=[srckey, k_mv, "lnc"], writes=[srckey])
        sc.add("dve", lambda e: e.scalar_tensor_tensor(out=dst[:, :], in0=src[:, :], scalar=rstd[:, 0:1], in1=bet[:, :],
                                                       op0=ALU.mult, op1=ALU.add), reads=[srckey, k_rs, "lnc"], writes=[dstkey])

    def phase_merge(self, li, seqs):
        s = 'x'.join(str(q) for q in seqs)
        nc, cfg = self.nc, self.cfg
        bmap = [(q, t) for q in seqs for t in range(cfg.seq_lens[q] // 512)]
        NB = len(bmap)
        mixers = cfg.mixers
        srcs = [("na", "aT", 0), ("hy", "zT", 1), ("swa", "cT", 2)]
        act_src = [m for m in srcs if m[0] in mixers]
        with ExitStack() as es:
            sb = lambda n, shp, dt: es.enter_context(nc.sbuf_tensor(f"E{li}{s}_{n}", shp, dt))
            ps = lambda n, shp, dt: es.enter_context(nc.psum_tensor(f"E{li}{s}_{n}", shp, dt))
            wbr = [sb(f"wbr{i}", [128, 4, 1024], BF16) for i in range(3)]
            wo = sb("wo", [128, 8, 1024], BF16)
            gam = sb("gam", [128, 1024], F32)
            bet = sb("bet", [128, 1024], F32)
            br = [[sb(f"br{i}_{b}", [128, 4, 512], BF16) for b in range(2)] for i in range(3)]
            gt = [sb(f"gt{b}", [128, 24, 512], BF16) for b in range(2)]
            xs = [sb(f"xs{i}", [128, 1024], F32) for i in range(4)]
            mg = [sb(f"mg{b}", [128, 8, 512], BF16) for b in range(2)]
            tmpa = [sb(f"tmpa{i}", [128, 512], F32) for i in range(2)]
            sres = [sb(f"sres{i}", [128, 1024], F32) for i in range(2)]
            xo = [sb(f"xo{i}", [128, 1024], F32) for i in range(2)]
            lnt = [(sb(f"lst{i}", [128, 12], F32), sb(f"lmv{i}", [128, 2], F32), sb(f"lrs{i}", [128, 1], F32)) for i in range(2)]
            acc = [ps(f"acc{i}", [128, 512], F32) for i in range(6)]
            sc = Sched(nc, f"E{li}{s}")
            self.epsc = sb("epsc", [128, 1], F32)
            sc.add("dve", lambda e: e.memset(self.epsc[:, :], LN_EPS), writes=["epsc"])
            for i in range(3):
                for k in range(4):
                    sc.dma("pool", wbr[i][:, k, :], self.w_br[i][li, :, k, :], writes=[Part(("wbr", i), k)])
            for k in range(8):
                sc.dma("pool", wo[:, k, :], self.w_out[li, :, k, :], writes=[Part("wo", k)])
            sc.dma("sp", gam[:, :], self.bvec("ln1_g", li), writes=[Part("lnc", 0)])
            sc.dma("sp", bet[:, :], self.bvec("ln1_b", li), writes=[Part("lnc", 1)])
            acci = 0
            for tb in range(NB):
                sq, ltb = bmap[tb]
                S = self.S[sq]
                x_src = self.x_in[sq] if li == 0 else S["xmid"]
                b = tb % 2
                c0 = ltb * 512
                for (mn, dn, i) in act_src:
                    sc.dma("sp", br[i][b][:, :, :], S[dn][:, :, c0:c0 + 512].rearrange("c p t -> p c t"), writes=[("br", i, b)])
                if act_src:
                    sc.dma("sp", gt[b][:, :, :], S["g"][:, :, c0:c0 + 512].rearrange("c p t -> p c t"), writes=[("gt", b)])
                for oc in range(8):
                    first = True
                    for (mn, dn, i) in act_src:
                        a = acc[acci % 6]
                        ka = ("acc", acci % 6)
                        acci += 1

                        def mm(e, a=a, i=i, oc=oc, b=b):
                            last = None
                            for k in range(4):
                                last = e.matmul(a[:, :], lhsT=wbr[i][:, k, oc * 128:(oc + 1) * 128], rhs=br[i][b][:, k, :],
                                                start=(k == 0), stop=(k == 3))
                            return last
                        sc.add("pe", mm, reads=[("wbr", i), ("br", i, b)], writes=[ka])
                        gsl = gt[b][:, i * 8 + oc, :]
                        kt = ("tmpa", oc % 2)
                        tm = tmpa[oc % 2]
                        last_one = (mn == act_src[-1][0])
                        dst = mg[b][:, oc, :] if last_one else tm[:, :]
                        dkey = Part(("mg", b), oc) if last_one else kt
                        if first:
                            sc.add("dve", lambda e, dst=dst, a=a, gsl=gsl: e.tensor_tensor(out=dst, in0=a[:, :], in1=gsl, op=ALU.mult),
                                   reads=[ka, ("gt", b)], writes=[dkey])
                            first = False
                        else:
                            kt2 = ("tmpb", oc % 2)
                            sc.add("dve", lambda e, a=a, gsl=gsl, oc=oc: e.tensor_tensor(out=sres[oc % 2][:, 0:512], in0=a[:, :], in1=gsl, op=ALU.mult),
                                   reads=[ka, ("gt", b)], writes=[kt2])
                            sc.add("pool", lambda e, dst=dst, tm=tm, oc=oc: e.tensor_tensor(out=dst, in0=tm[:, :], in1=sres[oc % 2][:, 0:512], op=ALU.add),
                                   reads=[kt, kt2], writes=[dkey])
                for t in range(4):
                    gi = tb * 4 + t
                    sl = gi % 2
                    xl = gi % 4
                    tok0 = c0 + t * 128
                    sc.dma("sp", xs[xl][:, :], x_src[tok0:tok0 + 128, :], writes=[("xs", xl)])
                    if act_src:
                        for h in range(2):
                            a = acc[acci % 6]
                            ka = ("acc", acci % 6)
                            acci += 1

                            def mm3(e, a=a, h=h, t=t, b=b):
                                last = None
                                for k in range(8):
                                    last = e.matmul(a[:, :], lhsT=mg[b][:, k, t * 128:(t + 1) * 128], rhs=wo[:, k, h * 512:(h + 1) * 512],
                                                    start=(k == 0), stop=(k == 7))
                                return last
                            sc.add("pe", mm3, reads=[("mg", b), "wo"], writes=[ka])
                            sc.add("dve", lambda e, a=a, h=h, xl=xl: e.scalar_tensor_tensor(
                                out=xs[xl][:, h * 512:(h + 1) * 512], in0=xs[xl][:, h * 512:(h + 1) * 512], scalar=ALPHA,
                                in1=a[:, :], op0=ALU.mult, op1=ALU.add), reads=[ka, ("xs", xl)], writes=[("xs", xl)])
                    else:
                        sc.add("dve", lambda e, xl=xl: e.tensor_scalar(out=xs[xl][:, :], in0=xs[xl][:, :], scalar1=ALPHA, scalar2=None,
                                                                       op0=ALU.mult), reads=[("xs", xl)], writes=[("xs", xl)])
                    self._ln(sc, None, xs[xl], ("xs", xl), xo[sl], ("xo", sl), gam, bet, lnt[sl], sl)
                    sc.dma("act", S["x1"][tok0:tok0 + 128, :], xo[sl][:, :], reads=[("xo", sl)])
            sc.run()

    def phase_mlp(self, li, seqs):
        s = 'x'.join(str(q) for q in seqs)
        nc, cfg = self.nc, self.cfg
        TB = 256
        bmap = [(q, t) for q in seqs for t in range(cfg.seq_lens[q] // TB)]
        NB = len(bmap)
        with ExitStack() as es:
            sb = lambda n, shp, dt: es.enter_context(nc.sbuf_tensor(f"F{li}{s}_{n}", shp, dt))
            ps = lambda n, shp, dt: es.enter_context(nc.psum_tensor(f"F{li}{s}_{n}", shp, dt))
            wu = sb("wu", [128, 8, 4096], BF16)
            wd = sb("wd", [128, 32, 1024], BF16)
            bu = sb("bu", [128, 32], F32)
            gam = sb("gam", [128, 1024], F32)
            bet = sb("bet", [128, 1024], F32)
            bdn = sb("bdn", [128, 1024], F32)
            ident = sb("ident", [128, 128], BF16)
            xs = [sb(f"xs{i}", [128, 1024], F32) for i in range(3)]
            xb = [sb(f"xb{i}", [128, 1024], BF16) for i in range(2)]
            xT = [sb(f"xT{i}", [128, 8, TB], BF16) for i in range(2)]
            hT = [sb(f"hT{i}", [128, 32, TB], BF16) for i in range(2)]
            yt = [sb(f"yt{i}", [128, TB], F32) for i in range(2)]
            xo = [sb(f"xo{i}", [128, 1024], F32) for i in range(2)]
            lnt = [(sb(f"lst{i}", [128, 12], F32), sb(f"lmv{i}", [128, 2], F32), sb(f"lrs{i}", [128, 1], F32)) for i in range(2)]
            pT = [ps(f"pT{i}", [128, 8, 128], BF16) for i in range(2)]
            acc = [ps(f"acc{i}", [128, 512], F32) for i in range(6)]
            sc = Sched(nc, f"F{li}{s}")
            self.epsc = sb("epsc", [128, 1], F32)
            sc.add("dve", lambda e: e.memset(self.epsc[:, :], LN_EPS), writes=["epsc"])
            for k in range(8):
                sc.dma("pool", wu[:, k, :], self.w_up[li, :, k, :], writes=[Part("wu", k)])
            for k in range(32):
                sc.dma("pool", wd[:, k, :], self.w_down[li, :, k, :], writes=[Part("wd", k)])
            sc.dma("sp", bu[:, :], self.b_up[li], writes=["bu"])
            sc.dma("sp", ident[:, :], self.ident[:, :], writes=["ident"])
            sc.dma("sp", gam[:, :], self.bvec("ln2_g", li), writes=[Part("lnc", 0)])
            sc.dma("sp", bet[:, :], self.bvec("ln2_b", li), writes=[Part("lnc", 1)])
            sc.dma("sp", bdn[:, :], self.bvec("b_down", li), writes=["bdn"])
            acci = 0
            NT = TB // 128
            xkeep = {}
            for tb in range(NB):
                sq, ltb = bmap[tb]
                S = self.S[sq]
                dst_t = self.y_out[sq] if li == cfg.depth - 1 else S["xmid"]
                b = tb % 2
                kx = ("xT", b)
                for t in range(NT):
                    gi = tb * NT + t
                    sl = gi % 2
                    xl = gi % 3
                    tok0 = ltb * TB + t * 128
                    sc.dma("sp", xs[xl][:, :], S["x1"][tok0:tok0 + 128, :], writes=[("xs", xl)])
                    sc.add("pool", lambda e, o=xb[sl], i=xs[xl]: e.tensor_copy(out=o[:, :], in_=i[:, :]),
                           reads=[("xs", xl)], writes=[("xb", sl)])

                    def tr(e, o=pT[sl], i=xb[sl]):
                        last = None
                        for k in range(8):
                            last = e.transpose(out=o[:, k, :], in_=i[:, k * 128:(k + 1) * 128], identity=ident[:, :])
                        return last
                    sc.add("pe", tr, reads=[("xb", sl), "ident"], writes=[("pT", sl)])
                    sc.add("dve", lambda e, o=xT[b], i=pT[sl], t=t: e.tensor_copy(out=o[:, :, t * 128:(t + 1) * 128], in_=i[:, :, :]),
                           reads=[("pT", sl)], writes=[Part(kx, t)])
                    sc.add("dve", lambda e, xl=xl: e.scalar_tensor_tensor(out=xs[xl][:, :], in0=xs[xl][:, :], scalar=ALPHA, in1=bdn[:, :],
                                                                          op0=ALU.mult, op1=ALU.add),
                           reads=[("xs", xl), ("xb", sl), "bdn"], writes=[("xs", xl)])
                for fc in range(32):
                    a = acc[acci % 6]
                    ka = ("acc", acci % 6)
                    acci += 1

                    def mm(e, a=a, fc=fc, b=b):
                        last = None
                        for k in range(8):
                            last = e.matmul(a[:, 0:TB], lhsT=wu[:, k, fc * 128:(fc + 1) * 128], rhs=xT[b][:, k, :],
                                            start=(k == 0), stop=(k == 7))
                        return last
                    sc.add("pe", mm, reads=["wu", kx], writes=[ka])
                    y = yt[fc % 2]
                    ky = ("yt", fc % 2)
                    sc.add("act", lambda e, y=y, a=a, fc=fc: e.activation(out=y[:, :], in_=a[:, 0:TB], func=AF.Relu,
                                                                          bias=bu[:, fc:fc + 1], scale=1.0),
                           reads=[ka, "bu"], writes=[ky])
                    eng = "dve" if fc % 2 == 0 else "pool"
                    sc.add(eng, lambda e, y=y, fc=fc, b=b: e.tensor_tensor(out=hT[b][:, fc, :], in0=y[:, :], in1=y[:, :], op=ALU.mult),
                           reads=[ky], writes=[Part(("hT", b), fc)])
                for t in range(NT):
                    gi = tb * NT + t
                    sl = gi % 2
                    xl = gi % 3
                    tok0 = ltb * TB + t * 128
                    for h in range(2):
                        a = acc[acci % 6]
                        ka = ("acc", acci % 6)
                        acci += 1

                        def mm3(e, a=a, h=h, t=t, b=b):
                            last = None
                            for k in range(32):
                                last = e.matmul(a[:, :], lhsT=hT[b][:, k, t * 128:(t + 1) * 128], rhs=wd[:, k, h * 512:(h + 1) * 512],
                                                start=(k == 0), stop=(k == 31))
                            return last
                        sc.add("pe", mm3, reads=[("hT", b), "wd"], writes=[ka])
                        sc.add("dve", lambda e, a=a, h=h, xl=xl: e.tensor_tensor(
                            out=xs[xl][:, h * 512:(h + 1) * 512], in0=xs[xl][:, h * 512:(h + 1) * 512], in1=a[:, :], op=ALU.add),
                            reads=[ka, ("xs", xl)], writes=[("xs", xl)])
                    self._ln(sc, None, xs[xl], ("xs", xl), xo[sl], ("xo", sl), gam, bet, lnt[sl], sl)
                    sc.dma("act", dst_t[tok0:tok0 + 128, :], xo[sl][:, :], reads=[("xo", sl)])
            sc.run()

    def phase_na(self, li, s):
        nc, cfg = self.nc, self.cfg
        L = cfg.seq_lens[s]
        S = self.S[s]
        rows = L // GRID_W
        with ExitStack() as es:
            sb = lambda n, shp, dt: es.enter_context(nc.sbuf_tensor(f"B{li}{s}_{n}", shp, dt))
            ps = lambda n, shp, dt: es.enter_context(nc.psum_tensor(f"B{li}{s}_{n}", shp, dt))
            ident = sb("ident", [128, 128], BF16)
            ones = sb("ones", [128, 128], BF16)
            t32 = sb("t32", [128, 8 * 8 * 64], F32)
            mk = sb("mk", [128, 64], F32)
            tt = [sb(f"tt{t}", [128, 8, 8, 64], BF16) for t in range(2)]
            qT = [sb(f"qT{i}", [128, 4, 512], BF16) for i in range(2)]
            kT = [sb(f"kT{i}", [128, 4, 512], BF16) for i in range(2)]
            vw = [sb(f"vw{i}", [128, 4, 512], BF16) for i in range(2)]
            P = [[sb(f"P{i}_{e}", [128, 2, 4, 64], BF16) for e in range(2)] for i in range(2)]
            rec = [sb(f"rec{i}", [128, 256], F32) for i in range(2)]
            ao = [sb(f"ao{i}", [128, 4, 512], BF16) for i in range(2)]
            Sp = [[ps(f"S{i}_{e}", [128, 2, 4, 64], F32) for e in range(2)] for i in range(2)]
            Op = [ps(f"O{i}", [128, 512], F32) for i in range(2)]
            sc = Sched(nc, f"B{li}{s}")
            sc.dma("sp", ident[:, :], self.ident[:, :], writes=["ident"])
            sc.dma("sp", mk[:, :], self.na_mask[:, :], writes=["mk"])
            sc.add("dve", lambda e: e.memset(ones[:, :], 1.0), writes=["ones"])
            for t in range(2):
                sc.dma("sp", t32[:, :], self.na_tab[li, t].rearrange("p h m w -> p (h m w)"), writes=["t32"])
                sc.add("dve", lambda e, t=t: e.scalar_tensor_tensor(
                    out=sbap(tt[t], 0, 128, 0, [[64, 64], [1, 64]]), in0=sbap(t32, 0, 128, 0, [[64, 64], [1, 64]]),
                    scalar=float(HEAD_DIM ** 0.5), in1=sbap(mk, 0, 128, 0, [[0, 64], [1, 64]]), op0=ALU.mult, op1=ALU.add),
                    reads=["t32", "mk"], writes=[("tt", t)])
            def loads(r):
                rs = min(max(r - NA_WIN_ROWS // 2, 0), rows - NA_WIN_ROWS)
                r8, rr = divmod(r, 8)
                qb = r8 % 2
                if rr == 0:
                    sc.dma("sp", qT[qb][:, :, :], S["qn"][:, :, r * 64:r * 64 + 512].rearrange("c p t -> p c t"), writes=[("qT", qb)])
                wb = r % 2
                sc.dma("sp", kT[wb][:, :, :], S["kn"][:, :, rs * 64:rs * 64 + 512].rearrange("c p t -> p c t"), writes=[("kT", wb)])
                sc.dma("sp", vw[wb][:, :, :], S["vn"][rs * 64:rs * 64 + 512, :].rearrange("(j p) f -> p j f", p=128), writes=[("vw", wb)])

            def s1(it):
                r, hq = divmod(it, 2)
                if hq == 0:
                    loads(r)
                rs = min(max(r - NA_WIN_ROWS // 2, 0), rows - NA_WIN_ROWS)
                dr = r - rs
                r8, rr = divmod(r, 8)
                qb = r8 % 2
                wb = r % 2
                if dr % 2 == 1:
                    tsel, m0 = 0, (7 - dr) // 2
                else:
                    tsel, m0 = 1, (6 - dr) // 2
                sb_i = it % 2
                Sb, Pb = Sp[sb_i], P[sb_i]

                def qk(e):
                    for ee in range(2):
                        for hpi in range(2):
                            h = 2 * (2 * hq + hpi) + ee
                            e.matmul(Sb[ee][:, hpi, :, :], lhsT=ident[:, :], rhs=tt[tsel][:, h, m0:m0 + 4, :], start=(hpi == 0), stop=False)
                    last = None
                    for ee in range(2):
                        for hpi in range(2):
                            hp = 2 * hq + hpi
                            for j in range(4):
                                last = e.matmul(Sb[ee][:, hpi, j, :], lhsT=kT[wb][64 * ee:64 * ee + 64, hp, j * 128:(j + 1) * 128],
                                                rhs=qT[qb][64 * ee:64 * ee + 64, hp, rr * 64:(rr + 1) * 64],
                                                start=False, stop=(hpi == 1 and j == 3))
                    return last
                sc.add("pe", qk, reads=["ident", ("tt", tsel), ("kT", wb), ("qT", qb)], writes=[("S", sb_i)])
                for ee in range(2):
                    sc.add("act", lambda e, ee=ee: e.activation(out=Pb[ee][:, :, :, :], in_=Sb[ee][:, :, :, :], func=AF.Exp,
                                                                 scale=float(HEAD_DIM ** -0.5)),
                           reads=[("S", sb_i)], writes=[Part(("P", sb_i), ee)])

            def s2(it):
                r, hq = divmod(it, 2)
                r8, rr = divmod(r, 8)
                wb = r % 2
                ab = r8 % 2
                sb_i = it % 2
                Pb, Ob, rb = P[sb_i], Op[sb_i], rec[sb_i]

                def pv(e):
                    for hpi in range(2):
                        hp = 2 * hq + hpi
                        for ee in range(2):
                            o0 = (hpi * 2 + ee) * 64
                            for j in range(4):
                                e.matmul(Ob[:, o0:o0 + 64], lhsT=vw[wb][:, j, hp * 128:(hp + 1) * 128], rhs=Pb[ee][:, hpi, j, :],
                                         start=(j == 0), stop=(j == 3))
                    last = None
                    for ee in range(2):
                        d0 = 256 + ee * 128
                        for j in range(4):
                            last = e.matmul(Ob[:, d0:d0 + 128], lhsT=ones[:, :], rhs=Pb[ee][:, :, j, :], start=(j == 0), stop=(j == 3))
                    return last
                sc.add("pe", pv, reads=[("P", sb_i), ("vw", wb), "ones"], writes=[("O", sb_i)])
                sc.add("dve", lambda e: e.reciprocal(out=rb[:, :], in_=Ob[:, 256:512]), reads=[("O", sb_i)], writes=[("rec", sb_i)])
                for hpi in range(2):
                    hp = 2 * hq + hpi
                    for ee in range(2):
                        o0 = (hpi * 2 + ee) * 64
                        d0 = (ee * 2 + hpi) * 64
                        sc.add("dve", lambda e, ee=ee, hp=hp, o0=o0, d0=d0: e.tensor_tensor(
                            out=ao[ab][64 * ee:64 * ee + 64, hp, rr * 64:(rr + 1) * 64], in0=Ob[64 * ee:64 * ee + 64, o0:o0 + 64],
                            in1=rb[64 * ee:64 * ee + 64, d0:d0 + 64], op=ALU.mult),
                            reads=[("O", sb_i), ("rec", sb_i)], writes=[Part(("ao", ab), (hp, rr, ee))])
                if rr == 7 and hq == 1:
                    sc.dma("pool", S["aT"][:, :, (r - 7) * 64:(r - 7) * 64 + 512].rearrange("c p t -> p c t"), ao[ab][:, :, :], reads=[("ao", ab)])

            NIT = rows * 2
            s1(0)
            for it in range(NIT):
                if it + 1 < NIT:
                    s1(it + 1)
                s2(it)
            sc.run()

    def phase_swa(self, li, s):
        nc, cfg = self.nc, self.cfg
        L = cfg.seq_lens[s]
        S = self.S[s]
        NBq = L // 128
        with ExitStack() as es:
            sb = lambda n, shp, dt: es.enter_context(nc.sbuf_tensor(f"C{li}{s}_{n}", shp, dt))
            ps = lambda n, shp, dt: es.enter_context(nc.psum_tensor(f"C{li}{s}_{n}", shp, dt))
            ident = sb("ident", [128, 128], BF16)
            ones = sb("ones", [128, 128], BF16)
            alh = sb("alh", [128, 8, 3, 128], BF16)
            all_ = sb("all", [128, 8, 3, 128], BF16)
            snk = sb("snk", [128, 8], F32)
            esk = sb("esk", [128, 8], F32)
            qT = [sb(f"qT{i}", [128, 4, 512], BF16) for i in range(2)]
            kT = [sb(f"kT{i}", [128, 2, 384], BF16) for i in range(2)]
            vw = [sb(f"vw{i}", [128, 3, 256], BF16) for i in range(2)]
            P = [sb(f"P{i}", [128, 2, 3, 128], BF16) for i in range(2)]
            rec = [sb(f"rec{i}", [128, 256], F32) for i in range(2)]
            co = [sb(f"co{i}", [128, 4, 512], BF16) for i in range(2)]
            Sp = [[ps(f"S{i}_{e}", [128, 4, 128], F32) for e in range(2)] for i in range(2)]
            Op = [ps(f"O{i}", [128, 512], F32) for i in range(2)]
            sc = Sched(nc, f"C{li}{s}")
            sc.dma("sp", ident[:, :], self.ident[:, :], writes=["ident"])
            sc.dma("sp", alh[:, :, :, :], self.al_hi[:, :, :, :], writes=["alh"])
            sc.dma("sp", all_[:, :, :, :], self.al_lo[:, :, :, :], writes=["all"])
            sc.dma("sp", snk[:, :], bass.AP(self.swa_sink.tensor, li * 8, [[0, 128], [1, 8]]), writes=["snk"])
            sc.add("act", lambda e: e.activation(out=esk[:, :], in_=snk[:, :], func=AF.Exp), reads=["snk"], writes=["esk"])
            sc.add("dve", lambda e: e.memset(ones[:, :], 1.0), writes=["ones"])
            def geom(bi):
                clo = 0 if bi > 0 else 1
                chi = 3 if bi < NBq - 1 else 2
                return clo, chi

            def loads(bi):
                b4, bb = divmod(bi, 4)
                qb = b4 % 2
                if bb == 0:
                    sc.dma("sp", qT[qb][:, :, :], S["qs"][:, :, bi * 128:bi * 128 + 512].rearrange("c p t -> p c t"), writes=[("qT", qb)])
                wb = bi % 2
                clo, chi = geom(bi)
                t0 = (bi - 1 + clo) * 128
                nt = (chi - clo) * 128
                sc.dma("sp", kT[wb][:, :, clo * 128:chi * 128], S["ks"][:, :, t0:t0 + nt].rearrange("c p t -> p c t"), writes=[("kT", wb)])
                sc.dma("sp", vw[wb][:, clo:chi, :], S["vs"][t0:t0 + nt, :].rearrange("(j p) f -> p j f", p=128), writes=[("vw", wb)])

            def s1(it):
                bi, hp = divmod(it, 4)
                if hp == 0:
                    loads(bi)
                b4, bb = divmod(bi, 4)
                qb = b4 % 2
                wb = bi % 2
                clo, chi = geom(bi)
                g = hp // 2
                sb_i = it % 2
                Pb = P[sb_i]
                for ee in range(2):
                    h = 2 * hp + ee
                    Sb = Sp[sb_i][ee]

                    def qk(e, Sb=Sb, h=h, ee=ee):
                        e.matmul(Sb[:, clo:chi, :], lhsT=ident[:, :], rhs=alh[:, h, clo:chi, :], start=True, stop=False)
                        e.matmul(Sb[:, clo:chi, :], lhsT=ident[:, :], rhs=all_[:, h, clo:chi, :], start=False, stop=False)
                        last = None
                        for c in range(clo, chi):
                            last = e.matmul(Sb[:, c, :], lhsT=kT[wb][64 * ee:64 * ee + 64, g, c * 128:(c + 1) * 128],
                                            rhs=qT[qb][64 * ee:64 * ee + 64, hp, bb * 128:(bb + 1) * 128],
                                            start=False, stop=(c == chi - 1))
                        return last
                    sc.add("pe", qk, reads=["ident", "alh", "all", ("kT", wb), ("qT", qb)], writes=[("S", sb_i, ee)])
                    sc.add("act", lambda e, Sb=Sb, ee=ee: e.activation(
                        out=Pb[:, ee, clo:chi, :], in_=Sb[:, clo:chi, :], func=AF.Exp, scale=float(HEAD_DIM ** -0.5)),
                        reads=[("S", sb_i, ee)], writes=[Part(("P", sb_i), ee)])

            def s2(it):
                bi, hp = divmod(it, 4)
                b4, bb = divmod(bi, 4)
                wb = bi % 2
                clo, chi = geom(bi)
                g = hp // 2
                sb_i = it % 2
                Pb, Ob, rb = P[sb_i], Op[sb_i], rec[sb_i]
                ab = b4 % 2

                def pv(e):
                    for ee in range(2):
                        for c in range(clo, chi):
                            e.matmul(Ob[:, ee * 128:(ee + 1) * 128], lhsT=vw[wb][:, c, g * 128:(g + 1) * 128], rhs=Pb[:, ee, c, :],
                                     start=(c == clo), stop=(c == chi - 1))
                    last = None
                    for c in range(clo, chi):
                        last = e.matmul(Ob[:, 256:512], lhsT=ones[:, :], rhs=Pb[:, :, c, :], start=(c == clo), stop=(c == chi - 1))
                    return last
                sc.add("pe", pv, reads=[("P", sb_i), ("vw", wb), "ones"], writes=[("O", sb_i)])
                for ee in range(2):
                    h = 2 * hp + ee
                    sc.add("dve", lambda e, ee=ee, h=h: e.tensor_scalar(
                        out=rb[:, ee * 128:(ee + 1) * 128], in0=Ob[:, 256 + ee * 128:256 + (ee + 1) * 128], scalar1=esk[:, h:h + 1], scalar2=None,
                        op0=ALU.add), reads=[("O", sb_i), "esk"], writes=[Part(("rec", sb_i), ee)])
                sc.add("dve", lambda e: e.reciprocal(out=rb[:, :], in_=rb[:, :]), reads=[("rec", sb_i)], writes=[("rec", sb_i)])
                for ee in range(2):
                    sc.add("dve", lambda e, ee=ee: e.tensor_tensor(
                        out=co[ab][64 * ee:64 * ee + 64, hp, bb * 128:(bb + 1) * 128], in0=Ob[64 * ee:64 * ee + 64, ee * 128:(ee + 1) * 128],
                        in1=rb[64 * ee:64 * ee + 64, ee * 128:(ee + 1) * 128], op=ALU.mult),
                        reads=[("O", sb_i), ("rec", sb_i)], writes=[Part(("co", ab), (hp, bb, ee))])
                if bb == 3 and hp == 3:
                    sc.dma("pool", S["cT"][:, :, (bi - 3) * 128:(bi - 3) * 128 + 512].rearrange("c p t -> p c t"), co[ab][:, :, :], reads=[("co", ab)])

            NIT = NBq * 4
            s1(0)
            for it in range(NIT):
                if it + 1 < NIT:
                    s1(it + 1)
                s2(it)
            sc.run()

    @staticmethod
    def hy_group(L):
        return 32 if L >= 8192 else 64

    def declare_hy_scratch(self):
        for L in self.cfg.ulens:
            d = self.HT[L]
            G = self.hy_group(L)
            d["G"] = G
            d["Hh"] = self.scratch(f"Hh{L}", [2, 512 // G, 128, d["NK1"], 2, G], BF16)
            d["Hs"] = self.scratch(f"Hs{L}", [2, 512 // G, 128, d["NK1"], 2, G], BF16)

    def phase_filt(self, li, L):
        nc = self.nc
        d = self.HT[L]
        TWO_PI = 2.0 * math.pi
        with ExitStack() as es:
            sb = lambda n, shp, dt: es.enter_context(nc.sbuf_tensor(f"FA{li}_{L}_{n}", shp, dt))
            ps = lambda n, shp, dt: es.enter_context(nc.psum_tensor(f"FA{li}_{L}_{n}", shp, dt))
            w1 = sb("w1", [33, 64], F32)
            w2 = sb("w2", [64, 64], F32)
            w3 = sb("w3", [64, 2048], BF16)
            b1 = sb("b1", [64, 1], F32)
            b2 = sb("b2", [64, 1], F32)
            fr = sb("fr", [64, 1], F32)
            fb1 = sb("fb1", [64, 1], F32)
            fb2 = sb("fb2", [64, 1], F32)
            negpi = sb("negpi", [64, 1], F32)
            adl = sb("adl", [128, 4], F32)
            zzb = [sb(f"zzb{i}", [33, 512], F32) for i in range(2)]
            tnb = [sb(f"tnb{i}", [128, 512], F32) for i in range(2)]
            a1 = [sb(f"a1_{i}", [64, 512], F32) for i in range(2)]
            t1 = [sb(f"t1_{i}", [64, 512], F32) for i in range(2)]
            t2 = [sb(f"t2_{i}", [64, 512], F32) for i in range(2)]
            h1 = [sb(f"h1_{i}", [64, 512], F32) for i in range(2)]
            h2 = [sb(f"h2_{i}", [64, 512], BF16) for i in range(2)]
            win = [[sb(f"win{i}_{c}", [128, 512], F32) for c in range(4)] for i in range(2)]
            stg = [sb(f"stg{i}", [128, 512], BF16) for i in range(4)]
            pm = [ps(f"pm{i}", [128, 512], F32) for i in range(2)]
            p3 = [ps(f"p3_{i}", [128, 512], F32) for i in range(4)]
            sc = Sched(nc, f"FA{li}_{L}")
            sc.dma("sp", w1[:, :], self.hy_w1[li], writes=["w1"])
            sc.dma("sp", w2[:, :], self.hy_w2[li], writes=["w2"])
            sc.dma("pool", w3[:, :], self.hy_w3[li], writes=["w3"])
            sc.dma("sp", b1[:, :], self.hy_b1[li], writes=["b1"])
            sc.dma("sp", b2[:, :], self.hy_b2[li], writes=["b2"])
            sc.dma("sp", fr[:, :], self.hy_fr[li], writes=["fr"])
            sc.dma("sp", adl[:, :], self.absdelta[:, :], writes=["adl"])
            sc.add("dve", lambda e: e.memset(negpi[:, :], -math.pi), writes=["negpi"])
            for (fb, bb, kf, kb) in ((fb1, b1, "fb1", "b1"), (fb2, b2, "fb2", "b2")):
                sc.add("dve", lambda e, fb=fb, bb=bb: e.tensor_tensor(out=fb[:, :], in0=fr[:, :], in1=bb[:, :], op=ALU.mult),
                       reads=["fr", kb], writes=[kf])
            NBp = 2 * L // 512
            si = 0
            pi3 = 0
            for nb in range(NBp):
                i = nb % 2
                n0 = nb * 512
                sc.dma("sp", zzb[i][:, :], d["zz"][:, n0:n0 + 512], writes=[("zzb", i)])
                sc.dma("sp", tnb[i][:, :], bass.AP(d["tneg"].tensor, n0, [[0, 128], [1, 512]]), writes=[("tnb", i)])
                sc.add("pe", lambda e, i=i: e.matmul(pm[0][0:64, :], lhsT=w1[:, :], rhs=zzb[i][:, :], start=True, stop=True),
                       reads=["w1", ("zzb", i)], writes=[("pm", 0)])
                for (stage, pmi, fb, kf, src_h, dst_h, kd) in ((0, 0, fb1, "fb1", None, h1, "h1"), (1, 1, fb2, "fb2", h1, h2, "h2")):
                    if stage == 1:
                        sc.add("pe", lambda e, i=i: e.matmul(pm[1][0:64, :], lhsT=w2[:, :], rhs=h1[i][:, :], start=True, stop=True),
                               reads=["w2", ("h1", i)], writes=[("pm", 1)])
                    sc.add("dve", lambda e, i=i, pmi=pmi, fb=fb: e.tensor_scalar(out=a1[i][:, :], in0=pm[pmi][0:64, :], scalar1=fr[:, 0:1], scalar2=fb[:, 0:1],
                                                                                op0=ALU.mult, op1=ALU.add),
                           reads=[("pm", pmi), "fr", kf], writes=[("a1", i)])
                    sc.add("dve", lambda e, i=i: e.tensor_scalar(out=t1[i][:, :], in0=a1[i][:, :], scalar1=math.pi, scalar2=-TWO_PI, op0=ALU.is_gt, op1=ALU.mult),
                           reads=[("a1", i)], writes=[("t1", i)])
                    sc.add("dve", lambda e, i=i: e.tensor_scalar(out=t2[i][:, :], in0=a1[i][:, :], scalar1=-math.pi, scalar2=TWO_PI, op0=ALU.is_lt, op1=ALU.mult),
                           reads=[("a1", i)], writes=[("t2", i)])
                    sc.add("dve", lambda e, i=i: e.tensor_tensor(out=a1[i][:, :], in0=a1[i][:, :], in1=t1[i][:, :], op=ALU.add),
                           reads=[("a1", i), ("t1", i)], writes=[("a1", i)])
                    sc.add("dve", lambda e, i=i: e.tensor_tensor(out=a1[i][:, :], in0=a1[i][:, :], in1=t2[i][:, :], op=ALU.add),
                           reads=[("a1", i), ("t2", i)], writes=[("a1", i)])
                    sc.add("act", lambda e, i=i, dst_h=dst_h: e.activation(out=dst_h[i][:, :], in_=a1[i][:, :], func=AF.Sin),
                           reads=[("a1", i)], writes=[(kd, i)])
                dirn = 0 if n0 < L else 1
                for cc in range(4):
                    sc.add("act", lambda e, i=i, cc=cc: e.activation(out=win[i][cc][:, :], in_=tnb[i][:, :], func=AF.Exp, scale=adl[:, cc:cc + 1]),
                           reads=[("tnb", i), "adl"], writes=[("win", i, cc)])
                for o in range(2):
                    for cc in range(4):
                        pb = pi3 % 4
                        pi3 += 1
                        col = (o * 2 + dirn) * 512 + cc * 128
                        sc.add("pe", lambda e, pb=pb, col=col, i=i: e.matmul(p3[pb][:, :], lhsT=w3[:, col:col + 128], rhs=h2[i][:, :], start=True, stop=True),
                               reads=["w3", ("h2", i)], writes=[("p3", pb)])
                        sb_i = si % 4
                        si += 1
                        sc.add("dve", lambda e, pb=pb, sb_i=sb_i, i=i, cc=cc: e.tensor_tensor(out=stg[sb_i][:, :], in0=p3[pb][:, :], in1=win[i][cc][:, :], op=ALU.mult),
                               reads=[("p3", pb), ("win", i, cc)], writes=[("stg", sb_i)])
                        sc.dma("sp", d["kT"][o * 4 + cc, :, n0:n0 + 512], stg[sb_i][:, :], reads=[("stg", sb_i)])
            sc.run()

    def _fft_consts(self, sc, sb, L):
        d = self.HT[L]
        N1, NK1, Na = d["N1"], d["NK1"], d["Na"]
        env = dict(d)
        env["F1t"] = sb("F1t", [N1, 2 * NK1], BF16)
        env["Mrt"] = sb("Mrt", [128, NK1, 128], BF16)
        env["Mit"] = sb("Mit", [128, NK1, 128], BF16)
        sc.dma("sp", env["F1t"][:, :], d["F1"][:, :], writes=["F1t"])
        sc.dma("sp", env["Mrt"][:, :, :], d["Mr"][:, :, :], writes=["Mrt"])
        sc.dma("sp", env["Mit"][:, :, :], d["Mi"][:, :, :], writes=["Mit"])
        return env

    def _fft_fwd(self, sc, env, X, xkey, Kp, consume, part=0):
        G, NK1 = env["G"], env["NK1"]
        W2 = 2 * NK1
        A, A2 = env["A"], env["A2"]
        AK, A2K = env.get("akey", "A"), env.get("a2key", "A2")
        F1t, Mrt, Mit = env["F1t"], env["Mrt"], env["Mit"]
        CB = max(1, 512 // W2)
        ci = 0
        for c0 in (range(0, G, CB) if part in (0, 1) else ()):
            ncb = min(CB, G - c0)
            bi = env["psA_i"][0] % 2
            env["psA_i"][0] += 1
            bank = env["psA"][bi]
            kb = ("psA", bi)

            def mm(e, bank=bank, c0=c0, ncb=ncb):
                last = None
                for j in range(ncb):
                    last = e.matmul(bank[:, j * W2:(j + 1) * W2], lhsT=X[0:Kp, c0 + j, :], rhs=F1t[0:Kp, :], start=True, stop=True)
                return last
            sc.add("pe", mm, reads=[xkey, "F1t"], writes=[kb])
            o1 = sbap(A, 0, 128, c0, [[1, ncb], [G, W2]])
            i1 = sbap(bank, 0, 128, 0, [[W2, ncb], [1, W2]])
            o2 = sbap(A2, 0, 128, c0, [[1, ncb], [2 * G, NK1]])
            i2 = sbap(bank, 0, 128, 1, [[W2, ncb], [2, NK1]])
            o3 = sbap(A2, 0, 128, c0 + G, [[1, ncb], [2 * G, NK1]])
            i3 = sbap(bank, 0, 128, 0, [[W2, ncb], [2, NK1]])
            if bi == 0:
                sc.add("act", lambda e, o1=o1, i1=i1: e.activation(out=o1, in_=i1, func=AF.Identity), reads=[kb], writes=[Part(AK, c0)])
                sc.add("act", lambda e, o2=o2, i2=i2: e.activation(out=o2, in_=i2, func=AF.Identity, scale=-1.0), reads=[kb], writes=[Part(A2K, (c0, 0))])
                sc.add("act", lambda e, o3=o3, i3=i3: e.activation(out=o3, in_=i3, func=AF.Identity), reads=[kb], writes=[Part(A2K, (c0, 1))])
            else:
                sc.add("dve", lambda e, o1=o1, i1=i1: e.tensor_copy(out=o1, in_=i1), reads=[kb], writes=[Part(AK, c0)])
                sc.add("dve", lambda e, o2=o2, i2=i2: e.tensor_scalar(out=o2, in0=i2, scalar1=-1.0, scalar2=None, op0=ALU.mult),
                       reads=[kb], writes=[Part(A2K, (c0, 0))])
                sc.add("dve", lambda e, o3=o3, i3=i3: e.tensor_copy(out=o3, in_=i3), reads=[kb], writes=[Part(A2K, (c0, 1))])
        KB = 512 // (2 * G)
        for k0 in (range(0, NK1, KB) if part in (0, 2) else ()):
            nk = min(KB, NK1 - k0)
            bi = env["psX_i"][0] % 2
            env["psX_i"][0] += 1
            bank = env["psX"][bi]
            kb = ("psX", bi)

            def mm2(e, bank=bank, k0=k0, nk=nk):
                last = None
                for kk in range(nk):
                    k1 = k0 + kk
                    e.matmul(bank[:, kk * 2 * G:(kk + 1) * 2 * G], lhsT=Mrt[:, k1, :], rhs=A[:, k1, :, :], start=True, stop=False)
                    last = e.matmul(bank[:, kk * 2 * G:(kk + 1) * 2 * G], lhsT=Mit[:, k1, :], rhs=A2[:, k1, :, :], start=False, stop=True)
                return last
            sc.add("pe", mm2, reads=[AK, A2K, "Mrt", "Mit"], writes=[kb])
            consume(k0, nk, bank, kb)

    def _fft_inv(self, sc, env, Y, ykey, out_cb, part=0):
        G, NK1, Na = env["G"], env["NK1"], env["Na"]
        Bp = env["Bp"]
        C1t, C2t, Trt, Tint = env["C1t"], env["C2t"], env["Trt"], env["Tint"]
        for cb in (range(G // 2) if part in (0, 1) else ()):
            c0 = 2 * cb
            bi = env["psB_i"][0] % 2
            env["psB_i"][0] += 1
            bank = env["psB"][bi]
            kb = ("psB", bi)

            def mm(e, bank=bank, c0=c0):
                last = None
                for j in range(2):
                    e.matmul(bank[0:NK1, j * 256:(j + 1) * 256], lhsT=Y[:, :, 0, c0 + j], rhs=C1t[:, :], start=True, stop=False)
                    last = e.matmul(bank[0:NK1, j * 256:(j + 1) * 256], lhsT=Y[:, :, 1, c0 + j], rhs=C2t[:, :], start=False, stop=True)
                return last
            sc.add("pe", mm, reads=[ykey, "C1t", "C2t"], writes=[kb])
            eng = "act" if bi == 0 else "dve"
            oap = sbap(Bp, 0, NK1, c0, [[1, 2], [G, 2], [2 * G, 128]])
            iap = sbap(bank, 0, NK1, 0, [[256, 2], [128, 2], [1, 128]])
            if eng == "act":
                sc.add("act", lambda e, oap=oap, iap=iap: e.activation(out=oap, in_=iap, func=AF.Identity), reads=[kb], writes=[Part("Bp", c0)])
            else:
                sc.add("dve", lambda e, oap=oap, iap=iap: e.tensor_copy(out=oap, in_=iap), reads=[kb], writes=[Part("Bp", c0)])
        BB = 512 // G
        for b0 in (range(0, 128, BB) if part in (0, 2) else ()):
            bi = env["psY_i"][0] % 2
            env["psY_i"][0] += 1
            bank = env["psY"][bi]
            kb = ("psY", bi)

            def mm2(e, bank=bank, b0=b0):
                last = None
                for bb in range(BB):
                    b = b0 + bb
                    e.matmul(bank[0:Na, bb * G:(bb + 1) * G], lhsT=Trt[0:NK1, b, :], rhs=Bp[0:NK1, b, 0, :], start=True, stop=False)
                    last = e.matmul(bank[0:Na, bb * G:(bb + 1) * G], lhsT=Tint[0:NK1, b, :], rhs=Bp[0:NK1, b, 1, :], start=False, stop=True)
                return last
            sc.add("pe", mm2, reads=["Bp", "Trt", "Tint"], writes=[kb])
            out_cb(b0, BB, bank, kb)

    def phase_fspec(self, li, L):
        nc = self.nc
        d = self.HT[L]
        G = d["G"]
        N1, NK1 = d["N1"], d["NK1"]
        NG = 512 // G
        KB = 512 // (2 * G)
        with ExitStack() as es:
            sb = lambda n, shp, dt: es.enter_context(nc.sbuf_tensor(f"FS{li}_{L}_{n}", shp, dt))
            ps = lambda n, shp, dt: es.enter_context(nc.psum_tensor(f"FS{li}_{L}_{n}", shp, dt))
            sc = Sched(nc, f"FS{li}_{L}")
            env = self._fft_consts(sc, sb, L)
            Ab = [sb(f"A_{i}", [128, NK1, 2, G], BF16) for i in range(2)]
            A2b = [sb(f"A2_{i}", [128, NK1, 2, G], BF16) for i in range(2)]
            env["psA"] = [ps(f"psA{i}", [128, 512], F32) for i in range(2)]
            env["psX"] = [ps(f"psX{i}", [128, 512], F32) for i in range(2)]
            env["psA_i"] = [0]
            env["psX_i"] = [0]
            X = [sb(f"X{i}", [N1, G, 128], BF16) for i in range(2)]
            bias = sb("bias", [128, 1024], F32)
            hst = [sb(f"hst{i}", [128, KB, 2, G], BF16) for i in range(2)]
            hss = [sb(f"hss{i}", [128, KB, 2, G], BF16) for i in range(2)]
            sc.dma("sp", bias[:, :], bass.AP(self.hy_bias.tensor, li * 1024, [[0, 128], [1, 1024]]), writes=["bias"])
            hi_ = [0]
            plist = [(o, g) for o in range(2) for g in range(NG)]

            def genv(i):
                e2 = dict(env)
                e2["A"], e2["A2"] = Ab[i % 2], A2b[i % 2]
                e2["akey"], e2["a2key"] = ("A", i % 2), ("A2", i % 2)
                return e2

            def stage(i, part):
                o, g = plist[i]
                xi = i % 2
                if part == 1:
                    ch, p0 = divmod(g * G, 128)
                    src = bass.AP(d["kT"].tensor, ((o * 4 + ch) * 128 + p0) * 2 * L, [[128, N1], [2 * L, G], [1, 128]])
                    sc.dma("sp", X[xi][:, :, :], src, writes=[("X", xi)])

                def consume(k0, nk, bank, kb):
                    hi = hi_[0] % 2
                    hi_[0] += 1
                    cg = o * 512 + g * G
                    pr = sbap(bank, 0, 128, 0, [[2 * G, nk], [1, G]])
                    pim = sbap(bank, 0, 128, G, [[2 * G, nk], [1, G]])
                    bb = sbap(bias, 0, 128, cg, [[0, nk], [1, G]])
                    sc.add("dve", lambda e: e.tensor_tensor(out=hst[hi][:, 0:nk, 0, :], in0=pr, in1=bb, op=ALU.add),
                           reads=[kb, "bias"], writes=[Part(("hst", hi), 0)])
                    sc.add("dve", lambda e: e.tensor_copy(out=hst[hi][:, 0:nk, 1, :], in_=pim),
                           reads=[kb], writes=[Part(("hst", hi), 1)])
                    sc.add("dve", lambda e: e.tensor_tensor(out=hss[hi][:, 0:nk, 1, :], in0=pr, in1=bb, op=ALU.add),
                           reads=[kb, "bias"], writes=[Part(("hss", hi), 1)])
                    sc.add("dve", lambda e: e.tensor_copy(out=hss[hi][:, 0:nk, 0, :], in_=pim),
                           reads=[kb], writes=[Part(("hss", hi), 0)])
                    sc.dma("sp", d["Hh"][o, g, :, k0:k0 + nk, :, :], hst[hi][:, 0:nk, :, :], reads=[("hst", hi)])
                    sc.dma("sp", d["Hs"][o, g, :, k0:k0 + nk, :, :], hss[hi][:, 0:nk, :, :], reads=[("hss", hi)])
                self._fft_fwd(sc, genv(i), X[xi], ("X", xi), N1, consume, part=part)

            NPASS = len(plist)
            stage(0, 1)
            for i in range(NPASS):
                if i + 1 < NPASS:
                    stage(i + 1, 1)
                stage(i, 2)
            sc.run()

    def phase_sconv(self, li, s):
        nc, cfg = self.nc, self.cfg
        L = cfg.seq_lens[s]
        S = self.S[s]
        PW = min(L, 2048)
        with ExitStack() as es:
            sb = lambda n, shp, dt: es.enter_context(nc.sbuf_tensor(f"SC{li}{s}_{n}", shp, dt))
            cw = sb("cw", [128, 12, 3], F32)
            cb = sb("cb", [128, 12], F32)
            hyc = [sb(f"hyc{i}", [128, L + 2], BF16) for i in range(2)]
            u32 = [sb(f"u32_{i}", [128, PW], F32) for i in range(2)]
            ub = [sb(f"ub{i}", [128, L], BF16) for i in range(2)]
            sc = Sched(nc, f"SC{li}{s}")
            sc.dma("sp", cw[:, :, :], self.hy_cw[li], writes=["cw"])
            sc.dma("sp", cb[:, :], self.hy_cb[li], writes=["cb"])
            pi = 0
            for j in range(12):
                i = j % 2
                sc.dma("sp", hyc[i][:, :], S["hy"][j], writes=[("hyc", i)])
                for t0 in range(0, L, PW):
                    ui = pi % 2
                    pi += 1
                    sc.add("act", lambda e, i=i, ui=ui, j=j, t0=t0: e.activation(out=u32[ui][:, :], in_=hyc[i][:, 1 + t0:1 + t0 + PW], func=AF.Identity,
                                                                              bias=cb[:, j:j + 1], scale=cw[:, j, 1:2]),
                           reads=[("hyc", i), "cw", "cb"], writes=[("u32", ui)])
                    sc.add("dve", lambda e, i=i, ui=ui, j=j, t0=t0: e.scalar_tensor_tensor(out=u32[ui][:, :], in0=hyc[i][:, t0:t0 + PW], scalar=cw[:, j, 0:1],
                                                                                       in1=u32[ui][:, :], op0=ALU.mult, op1=ALU.add),
                           reads=[("hyc", i), "cw", ("u32", ui)], writes=[("u32", ui)])
                    sc.add("dve", lambda e, i=i, ui=ui, j=j, t0=t0: e.scalar_tensor_tensor(out=ub[i][:, t0:t0 + PW], in0=hyc[i][:, 2 + t0:2 + t0 + PW], scalar=cw[:, j, 2:3],
                                                                                       in1=u32[ui][:, :], op0=ALU.mult, op1=ALU.add),
                           reads=[("hyc", i), "cw", ("u32", ui)], writes=[Part(("ub", i), t0)])
                sc.dma("sp", S["uT"][j], ub[i][:, :], reads=[("ub", i)])
            sc.run()

    def phase_hconv(self, li, s):
        nc, cfg = self.nc, self.cfg
        L = cfg.seq_lens[s]
        S = self.S[s]
        d = self.HT[L]
        G = d["G"]
        N1, NK1, Na = d["N1"], d["NK1"], d["Na"]
        NG = 512 // G
        KB = 512 // (2 * G)
        BB = 512 // G
        with ExitStack() as es:
            sb = lambda n, shp, dt: es.enter_context(nc.sbuf_tensor(f"HC{li}{s}_{n}", shp, dt))
            ps = lambda n, shp, dt: es.enter_context(nc.psum_tensor(f"HC{li}{s}_{n}", shp, dt))
            sc = Sched(nc, f"HC{li}{s}")
            env = self._fft_consts(sc, sb, L)
            env["A"] = sb("A", [128, NK1, 2, G], BF16)
            env["A2"] = sb("A2", [128, NK1, 2, G], BF16)
            env["Bp"] = sb("Bp", [NK1, 128, 2, G], BF16)
            env["C1t"] = sb("C1t", [128, 256], BF16)
            env["C2t"] = sb("C2t", [128, 256], BF16)
            env["Trt"] = sb("Trt", [NK1, 128, Na], BF16)
            env["Tint"] = sb("Tint", [NK1, 128, Na], BF16)
            sc.dma("sp", env["C1t"][:, :], self.C1[:, :], writes=["C1t"])
            sc.dma("sp", env["C2t"][:, :], self.C2[:, :], writes=["C2t"])
            sc.dma("sp", env["Trt"][:, :, :], d["Tr"][:, :, :], writes=["Trt"])
            sc.dma("sp", env["Tint"][:, :, :], d["Tin"][:, :, :], writes=["Tint"])
            for nm in ("psA", "psX", "psB", "psY"):
                env[nm] = [ps(f"{nm}{i}", [128, 512], F32) for i in range(2)]
                env[nm + "_i"] = [0]
            Xv = [sb(f"Xv{i}", [Na, G, 128], BF16) for i in range(2)]
            Xg = [sb(f"Xg{i}", [Na, G, 128], BF16) for i in range(2)]
            Z1 = [sb(f"Z1_{i}", [Na, G, 128], BF16) for i in range(2)]
            Hh = [sb(f"Hh{i}", [128, NK1, 2, G], BF16) for i in range(2)]
            Hs = [sb(f"Hs{i}", [128, NK1, 2, G], BF16) for i in range(2)]
            Y = [sb(f"Y{i}", [128, NK1, 2, G], BF16) for i in range(2)]
            P1 = [sb(f"P1_{i}", [128, KB, 2, G], F32) for i in range(2)]
            P2 = [sb(f"P2_{i}", [128, KB, 2, G], F32) for i in range(2)]
            pcnt = [0]

            def xload(dst, key, base_chunk, g):
                ch, p0 = divmod(g * G, 128)
                src = bass.AP(S["uT"].tensor, ((base_chunk + ch) * 128 + p0) * L, [[128, Na], [L, G], [1, 128]])
                sc.dma("sp", dst[:, :, :], src, writes=[key])

            passes = []
            for gp in range(0, NG, 2):
                for o in range(2):
                    for g in (gp, gp + 1):
                        if g < NG:
                            passes.append((g, o))

            def fwd(pi, part):
                g, o = passes[pi]
                vi = g % 2
                hi = pi % 2
                if part == 1:
                    if o == 0:
                        xload(Xv[vi], ("Xv", vi), 0, g)
                    xload(Xg[hi], ("Xg", hi), 4 * (o + 1), g)
                    sc.dma("sp", Hh[hi][:, :, :, :], d["Hh"][o, g], writes=[("Hh", hi)])
                    sc.dma("sp", Hs[hi][:, :, :, :], d["Hs"][o, g], writes=[("Hs", hi)])
                Xin, xkey = (Xv[vi], ("Xv", vi)) if o == 0 else (Z1[vi], ("Z1", vi))
                Yb = Y[hi]

                def consume(k0, nk, bank, kb):
                    pi_ = pcnt[0] % 2
                    pcnt[0] += 1
                    pb = sbap(bank, 0, 128, 0, [[1, nk * 2 * G]])
                    sc.add("dve", lambda e: e.tensor_tensor(
                        out=sbap(P1[pi_], 0, 128, 0, [[1, nk * 2 * G]]), in0=pb, in1=sbap(Hh[hi], 0, 128, k0 * 2 * G, [[1, nk * 2 * G]]), op=ALU.mult),
                        reads=[kb, ("Hh", hi)], writes=[("P1", pi_)])
                    sc.add("dve", lambda e: e.tensor_tensor(
                        out=sbap(P2[pi_], 0, 128, 0, [[1, nk * 2 * G]]), in0=pb, in1=sbap(Hs[hi], 0, 128, k0 * 2 * G, [[1, nk * 2 * G]]), op=ALU.mult),
                        reads=[kb, ("Hs", hi)], writes=[("P2", pi_)])
                    sc.add("pool", lambda e: e.tensor_tensor(
                        out=Yb[:, k0:k0 + nk, 0, :], in0=P1[pi_][:, 0:nk, 0, :], in1=P1[pi_][:, 0:nk, 1, :], op=ALU.subtract),
                        reads=[("P1", pi_)], writes=[Part(("Y", hi), (k0, 0))])
                    sc.add("pool", lambda e: e.tensor_tensor(
                        out=Yb[:, k0:k0 + nk, 1, :], in0=P2[pi_][:, 0:nk, 0, :], in1=P2[pi_][:, 0:nk, 1, :], op=ALU.add),
                        reads=[("P2", pi_)], writes=[Part(("Y", hi), (k0, 1))])
                self._fft_fwd(sc, env, Xin, xkey, Na, consume, part=part)

            def inv(pi, part):
                g, o = passes[pi]
                vi = g % 2
                hi = pi % 2
                if o == 0:
                    dst, dkey = Z1[vi], ("Z1", vi)
                else:
                    dst, dkey = Xv[vi], ("Xv", vi)

                def out_cb(b0, nb, bank, kb):
                    sc.add("dve", lambda e: e.tensor_tensor(
                        out=sbap(dst, 0, Na, b0, [[128, G], [1, nb]]), in0=sbap(bank, 0, Na, 0, [[1, G], [G, nb]]),
                        in1=sbap(Xg[hi], 0, Na, b0, [[128, G], [1, nb]]), op=ALU.mult),
                        reads=[kb, ("Xg", hi)], writes=[Part(dkey, b0)])
                self._fft_inv(sc, env, Y[hi], ("Y", hi), out_cb, part=part)
                if o == 1 and part == 2:
                    ch, p0 = divmod(g * G, 128)
                    dstap = bass.AP(S["zT"].tensor, (ch * 128 + p0) * L, [[128, Na], [L, G], [1, 128]])
                    sc.dma("act", dstap, Xv[vi][:, :, :], reads=[("Xv", vi)])

            NP = len(passes)
            fwd(0, 1)
            fwd(0, 2)
            for pi in range(NP):
                nxt = pi + 1 < NP
                if nxt:
                    fwd(pi + 1, 1)
                inv(pi, 1)
                if nxt:
                    fwd(pi + 1, 2)
                inv(pi, 2)
            sc.run()

    def build(self):
        cfg = self.cfg
        self.declare()
        only = getattr(cfg, "only", None)
        self.declare_hy_scratch()
        with ExitStack() as es:
            GLOB[0] = Sched.make_glob(self.nc, es)
            for li in range(cfg.depth):
                if "hy" in cfg.mixers:
                    for L in cfg.ulens:
                        if only is None or "D" in only or "D0" in only:
                            self.phase_filt(li, L)
                        if only is None or "D" in only or "D1" in only:
                            self.phase_fspec(li, L)
                allseq = list(range(len(cfg.seq_lens)))
                if only is None or "A" in only:
                    self.phase_inproj(li, allseq)
                for s in allseq:
                    if "na" in cfg.mixers and (only is None or "B" in only):
                        self.phase_na(li, s)
                    if "swa" in cfg.mixers and (only is None or "C" in only):
                        self.phase_swa(li, s)
                    if "hy" in cfg.mixers and (only is None or "D" in only or "D2" in only):
                        self.phase_sconv(li, s)
                    if "hy" in cfg.mixers and (only is None or "D" in only or "D3" in only):
                        self.phase_hconv(li, s)
                if only is None or "E" in only:
                    self.phase_merge(li, allseq)
                if only is None or "F" in only:
                    self.phase_mlp(li, allseq)
        return self.nc


def make_in_maps(cfg, inputs, n_cores, seq_arrays):
    wl = host_layout_weights(inputs, cfg.depth)
    consts = host_constants(cfg)
    maps = []
    for c in range(n_cores):
        m = dict(wl)
        m.update(consts)
        for s, x in enumerate(seq_arrays[c]):
            m[f"x{s}"] = np.ascontiguousarray(x, dtype=np.float32)
        maps.append(m)
    return maps


def kernel(**inputs):
    cfg = Cfg()
    b = Builder(cfg)
    nc = b.build()
    xp = np.asarray(inputs["x_prompt"], np.float32)
    xsm = np.asarray(inputs["x_sample"], np.float32)
    n = 8
    seqs = [[xp[c], xsm[2 * c], xsm[2 * c + 1]] for c in range(n)]
    in_maps = make_in_maps(cfg, inputs, n, seqs)
    res = run_bass_kernel_spmd(nc, in_maps, core_ids=list(range(n)))
    yp = np.stack([np.asarray(res.results[c]["y0"], np.float32) for c in range(n)])
    ys = np.stack([np.asarray(res.results[c][f"y{1 + j}"], np.float32) for c in range(n) for j in range(2)])
    return (yp, ys)
```
